# Optimizing a Trainium2 kernel written in Bass

```python
import jax, jax.numpy as jnp
from jax import lax
import numpy as np

D_MODEL = 2048
BATCH = 4
SEQ = 8192
DEPTH = 1

MIX_WIDTH = D_MODEL
HG_DK = 128
HG_DV = 128
HG_HEADS = (MIX_WIDTH // 2) // HG_DV
HG_K = HG_HEADS * HG_DK
HG_V = HG_HEADS * HG_DV
GDN_DK = 128
GDN_DV = 128
GDN_HEADS = (MIX_WIDTH - HG_V) // GDN_DV
GDN_K = GDN_HEADS * GDN_DK
GDN_V = GDN_HEADS * GDN_DV
CONV_W = 4
CHUNK = 64
D_FF = 4 * D_MODEL
N_DIR = 2
EPS = 1e-6
IN_SPLITS = (HG_K, HG_K, HG_K, HG_V, HG_V,
             GDN_K, GDN_K, GDN_V, GDN_V,
             N_DIR * GDN_HEADS, N_DIR * GDN_HEADS)
IN_COLS = sum(IN_SPLITS)

kernel_name = "hymba_style_hgrn2_gated_deltanet_encoder_layer"


def rmsnorm(x, w):
    xf = x.astype(jnp.float32)
    y = xf * lax.rsqrt(jnp.mean(xf * xf, axis=-1, keepdims=True) + EPS)
    return (y * w.astype(jnp.float32)).astype(x.dtype)


def l2norm(x):
    return x * lax.rsqrt(jnp.sum(x * x, axis=-1, keepdims=True) + EPS)


def to_chunks(x):
    b, t, h, d = x.shape
    return x.reshape(b, t // CHUNK, CHUNK, h, d).transpose(0, 3, 1, 2, 4)


def to_chunks_scalar(x):
    b, t, h = x.shape
    return x.reshape(b, t // CHUNK, CHUNK, h).transpose(0, 3, 1, 2)


def from_chunks(x):
    b, h, n, c, d = x.shape
    return x.transpose(0, 2, 3, 1, 4).reshape(b, n * c, h, d)


def short_conv(x, w):
    ch = x.shape[-1]
    left = CONV_W // 2
    return lax.conv_general_dilated(
        x, w[:, None, :].astype(x.dtype), window_strides=(1,),
        padding=[(left, CONV_W - 1 - left)],
        dimension_numbers=("NWC", "WIO", "NWC"), feature_group_count=ch)


def gla_chunkwise(q, k, v, log_f):
    c = q.shape[-2]
    b = jnp.cumsum(log_f, axis=-2)
    b_last = b[..., -1:, :]
    ref = b[..., c // 2 - 1:c // 2, :]
    tril = jnp.tril(jnp.ones((c, c), dtype=bool))
    a = jnp.einsum("bhntd,bhnsd->bhnts", q * jnp.exp(b - ref), k * jnp.exp(ref - b))
    a = jnp.where(tril, a, 0.0)
    o_intra = jnp.einsum("bhnts,bhnsv->bhntv", a, v)
    q_inter = q * jnp.exp(b)
    k_state = k * jnp.exp(b_last - b)
    chunk_decay = jnp.exp(b_last[..., 0, :])
    xs = (jnp.moveaxis(q_inter, 2, 0), jnp.moveaxis(k_state, 2, 0),
          jnp.moveaxis(v, 2, 0), jnp.moveaxis(chunk_decay, 2, 0))

    def step(s, inp):
        q_n, k_n, v_n, d_n = inp
        o_n = jnp.einsum("bhtd,bhdv->bhtv", q_n, s)
        s = s * d_n[..., None] + jnp.einsum("bhsd,bhsv->bhdv", k_n, v_n)
        return s, o_n

    bsz, h, _, _, dk = q.shape
    s0 = jnp.zeros((bsz, h, dk, v.shape[-1]), q.dtype)
    _, o_inter = lax.scan(step, s0, xs)
    return o_intra + jnp.moveaxis(o_inter, 0, 2)


def gated_delta_chunkwise(q, k, v, g, beta):
    c = q.shape[-2]
    gc = jnp.cumsum(g, axis=-1)
    tril = jnp.tril(jnp.ones((c, c), dtype=bool))
    strict = jnp.tril(jnp.ones((c, c), dtype=bool), -1)
    decay = jnp.exp(jnp.where(tril, gc[..., :, None] - gc[..., None, :], -jnp.inf))
    k_beta = k * beta[..., None]
    kk = jnp.einsum("bhntd,bhnsd->bhnts", k_beta, k) * decay
    lower = jnp.eye(c, dtype=q.dtype) + jnp.where(strict, kk, 0.0)
    rhs = jnp.concatenate([v * beta[..., None], k_beta * jnp.exp(gc)[..., None]], axis=-1)
    sol = lax.linalg.triangular_solve(lower, rhs, left_side=True, lower=True)
    dv = v.shape[-1]
    u, w = sol[..., :dv], sol[..., dv:]
    qk = jnp.where(tril, jnp.einsum("bhntd,bhnsd->bhnts", q, k) * decay, 0.0)
    q_inter = q * jnp.exp(gc)[..., None]
    k_state = k * jnp.exp(gc[..., -1:] - gc)[..., None]
    chunk_decay = jnp.exp(gc[..., -1])
    xs = (jnp.moveaxis(u, 2, 0), jnp.moveaxis(w, 2, 0), jnp.moveaxis(q_inter, 2, 0),
          jnp.moveaxis(qk, 2, 0), jnp.moveaxis(k_state, 2, 0), jnp.moveaxis(chunk_decay, 2, 0))

    def step(s, inp):
        u_n, w_n, q_n, qk_n, k_n, d_n = inp
        v_new = u_n - jnp.einsum("bhcd,bhdv->bhcv", w_n, s)
        o_n = jnp.einsum("bhcd,bhdv->bhcv", q_n, s) + jnp.einsum("bhts,bhsv->bhtv", qk_n, v_new)
        s = s * d_n[..., None, None] + jnp.einsum("bhsd,bhsv->bhdv", k_n, v_new)
        return s, o_n

    bsz, h, _, _, dk = q.shape
    s0 = jnp.zeros((bsz, h, dk, dv), q.dtype)
    _, o = lax.scan(step, s0, xs)
    return jnp.moveaxis(o, 0, 2)


def hgrn2_mixer(q_lin, f_fwd, f_bwd, i_val, g_out, lb_fwd, lb_bwd, norm_w):
    bsz, t, _ = q_lin.shape
    q = (jax.nn.silu(q_lin.astype(jnp.float32)) * HG_DK ** -0.5).reshape(bsz, t, HG_HEADS, HG_DK)
    v = i_val.astype(jnp.float32).reshape(bsz, t, HG_HEADS, HG_DV)

    def run(f_logits, lb, reverse):
        f_logits = f_logits.astype(jnp.float32)
        lb = lb.astype(jnp.float32)
        log_f = jnp.log(lb + (1.0 - lb) * jax.nn.sigmoid(f_logits))
        k = (1.0 - lb) * jax.nn.sigmoid(-f_logits)
        log_f = log_f.reshape(bsz, t, HG_HEADS, HG_DK)
        k = k.reshape(bsz, t, HG_HEADS, HG_DK)
        qq, vv = q, v
        if reverse:
            qq, k, vv, log_f = (jnp.flip(a, axis=1) for a in (qq, k, vv, log_f))
        o = from_chunks(gla_chunkwise(to_chunks(qq), to_chunks(k), to_chunks(vv), to_chunks(log_f)))
        return jnp.flip(o, axis=1) if reverse else o

    o = run(f_fwd, lb_fwd, False) + run(f_bwd, lb_bwd, True)
    gate = jax.nn.silu(g_out.astype(jnp.float32)).reshape(bsz, t, HG_HEADS, HG_DV)
    o = rmsnorm(o, norm_w) * gate
    return o.reshape(bsz, t, HG_V).astype(q_lin.dtype)


def gdn_mixer(q_lin, k_lin, v_lin, z, beta_logits, a_logits, conv_w, a_log, dt_bias, norm_w):
    bsz, t, _ = q_lin.shape
    qkv = jax.nn.silu(short_conv(jnp.concatenate([q_lin, k_lin, v_lin], axis=-1), conv_w))
    qkv = qkv.astype(jnp.float32)
    q, k, v = jnp.split(qkv, [GDN_K, 2 * GDN_K], axis=-1)
    q = l2norm(q.reshape(bsz, t, GDN_HEADS, GDN_DK)) * GDN_DK ** -0.5
    k = l2norm(k.reshape(bsz, t, GDN_HEADS, GDN_DK))
    v = v.reshape(bsz, t, GDN_HEADS, GDN_DV)
    beta = jax.nn.sigmoid(beta_logits.astype(jnp.float32)).reshape(bsz, t, N_DIR, GDN_HEADS)
    g = -jnp.exp(a_log.astype(jnp.float32)) * jax.nn.softplus(
        a_logits.astype(jnp.float32).reshape(bsz, t, N_DIR, GDN_HEADS) + dt_bias.astype(jnp.float32))

    def run(d, reverse):
        qq, kk, vv, bb, gg = q, k, v, beta[:, :, d], g[:, :, d]
        if reverse:
            qq, kk, vv, bb, gg = (jnp.flip(a, axis=1) for a in (qq, kk, vv, bb, gg))
        o = from_chunks(gated_delta_chunkwise(to_chunks(qq), to_chunks(kk), to_chunks(vv),
                                              to_chunks_scalar(gg), to_chunks_scalar(bb)))
        return jnp.flip(o, axis=1) if reverse else o

    o = run(0, False) + run(1, True)
    gate = jax.nn.silu(z.astype(jnp.float32)).reshape(bsz, t, GDN_HEADS, GDN_DV)
    o = rmsnorm(o, norm_w) * gate
    return o.reshape(bsz, t, GDN_V).astype(q_lin.dtype)


def setup_inputs(seed: int = 0) -> dict:
    key = jax.random.key(seed)
    ks = jax.random.split(key, 16)
    f32 = jnp.float32
    nrm = lambda k, shape, s: jax.random.normal(k, shape, f32) * s
    dt = jnp.exp(jax.random.uniform(ks[9], (DEPTH, N_DIR, GDN_HEADS), f32,
                                    minval=float(np.log(1e-3)), maxval=float(np.log(1e-1))))
    return {
        "x": nrm(ks[0], (BATCH, SEQ, D_MODEL), 1.0),
        "norm1_w": 1.0 + nrm(ks[1], (DEPTH, D_MODEL), 0.01),
        "w_in": nrm(ks[2], (DEPTH, D_MODEL, IN_COLS), D_MODEL ** -0.5),
        "hg_lb_logits_fwd": nrm(ks[3], (DEPTH + 1, HG_K), 0.1),
        "hg_lb_logits_bwd": nrm(ks[4], (DEPTH + 1, HG_K), 0.1),
        "hg_norm_w": 1.0 + nrm(ks[5], (DEPTH, HG_DV), 0.01),
        "gdn_conv_w": nrm(ks[6], (DEPTH, CONV_W, 2 * GDN_K + GDN_V), CONV_W ** -0.5),
        "gdn_a_log": jnp.log(jax.random.uniform(ks[7], (DEPTH, N_DIR, GDN_HEADS), f32, minval=1.0, maxval=16.0)),
        "gdn_dt_bias": dt + jnp.log(-jnp.expm1(-dt)),
        "gdn_norm_w": 1.0 + nrm(ks[8], (DEPTH, GDN_DV), 0.01),
        "w_out": nrm(ks[10], (DEPTH, MIX_WIDTH, D_MODEL), MIX_WIDTH ** -0.5),
        "norm2_w": 1.0 + nrm(ks[11], (DEPTH, D_MODEL), 0.01),
        "w_mlp_in": nrm(ks[12], (DEPTH, D_MODEL, D_FF), D_MODEL ** -0.5),
        "w_mlp_out": nrm(ks[13], (DEPTH, D_FF, D_MODEL), D_FF ** -0.5),
        "final_norm_w": 1.0 + nrm(ks[14], (D_MODEL,), 0.01),
    }


def reference(x, norm1_w, w_in, hg_lb_logits_fwd, hg_lb_logits_bwd, hg_norm_w, gdn_conv_w,
              gdn_a_log, gdn_dt_bias, gdn_norm_w, w_out, norm2_w, w_mlp_in, w_mlp_out,
              final_norm_w):
    split_idx = [int(i) for i in np.cumsum(IN_SPLITS)[:-1]]
    lb_tab_fwd = jnp.cumsum(jax.nn.softmax(hg_lb_logits_fwd.astype(jnp.float32), axis=0), axis=0)
    lb_tab_bwd = jnp.cumsum(jax.nn.softmax(hg_lb_logits_bwd.astype(jnp.float32), axis=0), axis=0)
    for layer in range(DEPTH):
        h = rmsnorm(x, norm1_w[layer])
        u = jnp.einsum("btd,dc->btc", h, w_in[layer])
        (hq, hf_f, hf_b, hi, hg, gq, gk, gv, gz, gbeta, ga) = jnp.split(u, split_idx, axis=-1)
        o_hg = hgrn2_mixer(hq, hf_f, hf_b, hi, hg, lb_tab_fwd[layer], lb_tab_bwd[layer],
                           hg_norm_w[layer])
        o_gdn = gdn_mixer(gq, gk, gv, gz, gbeta, ga, gdn_conv_w[layer], gdn_a_log[layer],
                          gdn_dt_bias[layer], gdn_norm_w[layer])
        mix = jnp.concatenate([o_hg, o_gdn], axis=-1)
        x = x + jnp.einsum("btc,cd->btd", mix, w_out[layer])
        h2 = rmsnorm(x, norm2_w[layer])
        hid = jnp.square(jax.nn.relu(jnp.einsum("btd,df->btf", h2, w_mlp_in[layer])))
        x = x + jnp.einsum("btf,fd->btd", hid, w_mlp_out[layer])
    return rmsnorm(x, final_norm_w)
```

```python
import numpy as np
from contextlib import ExitStack
import concourse.bass as bass
import concourse.mybir as mybir
from concourse.bass_utils import run_bass_kernel_spmd

F32 = mybir.dt.float32
BF16 = mybir.dt.bfloat16
ALU = mybir.AluOpType
AF = mybir.ActivationFunctionType
AX = mybir.AxisListType

D = 2048
NCOL = 9248
DFF = 8192
EPS = 1e-6
TB = 512

C_ID, C_ONES = 0, 128
C_U2, C_M1, C_M2, C_UD, C_VD = 0, 128, 256, 384, 512
C_DIR = {"A": 256, "B": 256 + 640}
C_MD16 = 256 + 1280
C_MOFF = {32: C_MD16 + 128, 64: C_MD16 + 256, 128: C_MD16 + 384}
NCONST = 256 + 1280 + 512


def make_consts():
    c = np.zeros((128, NCONST), np.float32)
    c[:, C_ID:C_ID + 128] = np.eye(128, dtype=np.float32)
    c[:, C_ONES:C_ONES + 128] = 1.0
    s = np.arange(128)[:, None]
    t = np.arange(128)[None, :]
    same = (s // 64) == (t // 64)
    c[:, C_MD16:C_MD16 + 128] = (s // 16) == (t // 16)
    for B_ in (32, 64, 128):
        c[:, C_MOFF[B_]:C_MOFF[B_] + 128] = ((s // B_) == (t // B_)) & ((s // (B_ // 2)) != (t // (B_ // 2)))
    for d in ("A", "B"):
        b = C_DIR[d]
        if d == "A":
            u2 = same & (s <= t)
            m1 = u2.astype(np.float32) - (same & ((s % 64) <= 31)).astype(np.float32)
            m2 = same & (s > t)
            ud = s <= t
            vd = s > t
        else:
            u2 = same & (s >= t)
            m1 = u2.astype(np.float32) - (same & ((s % 64) >= 32)).astype(np.float32)
            m2 = same & (s < t)
            ud = s >= t
            vd = s < t
        c[:, b + C_U2:b + C_U2 + 128] = u2
        c[:, b + C_M1:b + C_M1 + 128] = m1
        c[:, b + C_M2:b + C_M2 + 128] = m2
        c[:, b + C_UD:b + C_UD + 128] = ud
        c[:, b + C_VD:b + C_VD + 128] = vd
    return c


class Buf:
    __slots__ = ("w", "r", "excl")

    def __init__(self, excl=False):
        self.w = {}
        self.r = {}
        self.excl = excl


class V:
    __slots__ = ("ap", "bufs")

    def __init__(self, ap, bufs):
        self.ap = ap
        self.bufs = tuple(bufs)

    def __getitem__(self, k):
        return V(self.ap[k], self.bufs)

    def bc(self, shape):
        return V(self.ap.to_broadcast(shape), self.bufs)

    def un(self, ax):
        return V(self.ap.unsqueeze(ax), self.bufs)

    def re(self, s, **kw):
        return V(self.ap.rearrange(s, **kw), self.bufs)

    def bitcast(self, dt):
        return V(self.ap.bitcast(dt), self.bufs)


class Rot:
    def __init__(self, items):
        self.items = items
        self.i = 0

    def next(self):
        v = self.items[self.i % len(self.items)]
        self.i += 1
        return v


ENGS = ("pe", "act", "dve", "pool", "sp")
DEBUG_LABELS = False
GRP_HG = 4
GRP_GDN = 4
LABELS = {}
NSLOT = {"sp": 12, "pool": 12}


class Prog:
    def __init__(self, nc):
        self.nc = nc
        self.lists = {e: [] for e in ENGS}
        self.cnt = {e: 0 for e in ENGS}
        self.seen = {e: {} for e in ENGS}
        self.slot_val = {}
        self.slot_i = {"sp": 0, "pool": 0}
        self.es = ExitStack()

    def sb(self, name, shape, dt, nbuf=1):
        t = self.es.enter_context(self.nc.sbuf_tensor(name, list(shape), dt))
        return V(t[:], [Buf() for _ in range(nbuf)])

    def sbn(self, name, shape, dt, n):
        return [self.sb(f"{name}{i}", shape, dt) for i in range(n)]

    def arena_init(self, cols):
        self.arena = self.sb("arena", [128, cols], F32)
        self.acols = cols
        self.aoff = 0

    def al(self, shape, dt):
        n = 1
        for d_ in shape[1:]:
            n *= d_
        nb = n * (4 if dt == F32 else 2)
        nb = (nb + 63) // 64 * 64
        cols = nb // 4
        assert self.aoff + cols <= self.acols, ("arena overflow", self.aoff, cols, self.acols)
        ap = self.arena.ap[:, self.aoff:self.aoff + cols]
        if dt != F32:
            ap = ap.bitcast(dt)
        ap = ap[:, 0:n]
        if len(shape) == 3:
            ap = ap.rearrange("p (a b) -> p a b", a=shape[1])
        self.aoff += cols
        return V(ap, [Buf()])

    def aln(self, shape, dt, n):
        return [self.al(shape, dt) for _ in range(n)]

    def barrier(self):
        for e in ENGS:
            waits = []
            sn = self.seen[e]
            for k in ENGS:
                if k in ("sp", e) or self.cnt[k] == 0:
                    continue
                if sn.get(k, 0) < self.cnt[k]:
                    sn[k] = self.cnt[k]
                    waits.append((k, self.cnt[k]))
            for slot, val in self.slot_val.items():
                if sn.get(slot, 0) < val:
                    sn[slot] = val
                    waits.append((slot, val))
            self.lists[e].append((waits, None, None, None))

    def dram(self, name, shape, dt, kind="Internal"):
        t = self.nc.dram_tensor(name, list(shape), dt, kind=kind)
        return V(t.ap(), [Buf()])

    def _deps(self, eng, reads, writes):
        need = {}

        def req(key, val, war=False):
            if key == eng and (eng == "pe" or war):
                return
            if need.get(key, 0) < val:
                need[key] = val

        for v in reads:
            for b in v.bufs:
                for k, val in b.w.items():
                    req(k, val)
                if b.excl:
                    for k, val in b.r.items():
                        req(k, val, war=True)
        for v in writes:
            for b in v.bufs:
                for k, val in b.w.items():
                    req(k, val)
                for k, val in b.r.items():
                    req(k, val, war=True)
        waits = []
        sn = self.seen[eng]
        for key, val in need.items():
            if sn.get(key, 0) >= val:
                continue
            sn[key] = val
            waits.append((key, val))
        return waits

    def _mark(self, key, val, reads, writes):
        for v in reads:
            for b in v.bufs:
                b.r[key] = val
        for v in writes:
            for b in v.bufs:
                b.w[key] = val
                b.r = {}

    def op(self, eng, fn, reads, writes):
        if DEBUG_LABELS:
            import sys as _s
            f = _s._getframe(2)
            f2 = f.f_back
            fn._lbl = f"{f.f_code.co_name}:{f.f_lineno}<{f2.f_code.co_name}:{f2.f_lineno}"
        waits = self._deps(eng, reads, writes)
        self.cnt[eng] += 1
        n = self.cnt[eng]
        self._mark(eng, n, reads, writes)
        self.lists[eng].append((waits, fn, (eng, 1), n))

    def dma(self, q, out, in_):
        reads, writes = [in_], [out]
        waits = self._deps(q, reads, writes)
        slot = (q, self.slot_i[q] % NSLOT[q])
        self.slot_i[q] += 1
        prev = self.slot_val.get(slot, 0)
        if prev > 0 and self.seen[q].get(slot, 0) < prev:
            self.seen[q][slot] = prev
            waits.append((slot, prev))
        new = prev + 16
        self.slot_val[slot] = new
        self._mark(slot, new, reads, writes)
        oa, ia = out.ap, in_.ap
        self.lists[q].append((waits, lambda e: e.dma_start(out=oa, in_=ia), (slot, 16), None))

    def mm(self, out, lhsT, rhs, start=True, stop=True):
        oa, la, ra = out.ap, lhsT.ap, rhs.ap
        self.op("pe", lambda e: e.matmul(oa, la, ra, start=start, stop=stop), [lhsT, rhs], [out])

    def tr(self, out, in_, ident):
        oa, ia, da = out.ap, in_.ap, ident.ap
        self.op("pe", lambda e: e.transpose(oa, ia, da), [in_, ident], [out])

    def act(self, out, in_, func, bias=None, scale=None, accum=None):
        oa, ia = out.ap, in_.ap
        kw = {}
        reads = [in_]
        writes = [out]
        if bias is not None:
            if isinstance(bias, V):
                kw["bias"] = bias.ap
                reads.append(bias)
            else:
                kw["bias"] = float(bias)
        if scale is not None:
            if isinstance(scale, V):
                kw["scale"] = scale.ap
                reads.append(scale)
            else:
                kw["scale"] = float(scale)
        if accum is not None:
            kw["accum_out"] = accum.ap
            writes.append(accum)
        self.op("act", lambda e: e.activation(oa, ia, func, **kw), reads, writes)

    def tt(self, eng, out, a, b, op):
        oa, aa, ba = out.ap, a.ap, b.ap
        self.op(eng, lambda e: e.tensor_tensor(oa, aa, ba, op), [a, b], [out])

    def ts(self, eng, out, a, s1, s2, op0, op1=None):
        oa, aa = out.ap, a.ap
        reads = [a]
        x1 = s1
        x2 = s2
        if isinstance(s1, V):
            x1 = s1.ap
            reads.append(s1)
        if isinstance(s2, V):
            x2 = s2.ap
            reads.append(s2)
        if op1 is None:
            if op0 == ALU.pow:
                self.act(out, a, AF.Ln)
                self.act(out, out, AF.Exp, scale=float(x1))
                return
            if eng == "pool":
                eng = "dve"
            self.op(eng, lambda e: e.tensor_scalar(oa, aa, x1, None, op0), reads, [out])
        elif op1 == ALU.pow:
            if eng == "pool":
                eng = "dve"
            self.op(eng, lambda e: e.tensor_scalar(oa, aa, x1, None, op0), reads, [out])
            if x2 == -1.0:
                self.op("dve", lambda e: e.reciprocal(oa, oa), [out], [out])
            else:
                self.act(out, out, AF.Ln)
                self.act(out, out, AF.Exp, scale=float(x2))
        else:
            self.op(eng, lambda e: e.tensor_scalar(oa, aa, x1, x2, op0, op1), reads, [out])

    def stt(self, eng, out, a, s, b, op0, op1):
        oa, aa, ba = out.ap, a.ap, b.ap
        reads = [a, b]
        x = s
        if isinstance(s, V):
            x = s.ap
            reads.append(s)
        self.op(eng, lambda e: e.scalar_tensor_tensor(oa, aa, x, ba, op0, op1), reads, [out])

    def cp(self, eng, out, in_):
        oa, ia = out.ap, in_.ap
        if eng == "act":
            self.op("act", lambda e: e.copy(oa, ia), [in_], [out])
        else:
            self.op(eng, lambda e: e.tensor_copy(oa, ia), [in_], [out])

    def memset(self, eng, out, val):
        oa = out.ap
        self.op(eng, lambda e: e.memset(oa, val), [], [out])

    def red(self, eng, out, in_, op=ALU.add):
        oa, ia = out.ap, in_.ap
        self.op(eng, lambda e: e.tensor_reduce(oa, ia, AX.X, op), [in_], [out])

    def emit(self):
        nc = self.nc
        sems = {}
        for e in ENGS:
            if e != "sp":
                sems[e] = self.es.enter_context(nc.semaphore(f"s_{e}"))
        for slot in self.slot_val:
            sems[slot] = self.es.enter_context(nc.semaphore(f"d_{slot[0]}{slot[1]}"))
        final = [(sems[s], v) for s, v in self.slot_val.items()]
        lists = self.lists

        waited = {e: set() for e in ENGS}
        for e in ENGS:
            for waits, fn, kk, n in lists[e]:
                for key, val in waits:
                    if key in waited:
                        waited[key].add(val)
        rank = {e: {n: i + 1 for i, n in enumerate(sorted(waited[e]))} for e in ENGS}

        ctr = [0]

        def replay(name, eng, tail=False):
            for waits, fn, kk, n in lists[name]:
                for key, val in waits:
                    if key in rank:
                        val = rank[key][val]
                    eng.wait_ge(sems[key], val)
                    ctr[0] += 1
                if fn is None:
                    continue
                k, inc = kk
                ins = fn(eng)
                if DEBUG_LABELS:
                    LABELS[ctr[0]] = (name, getattr(fn, "_lbl", "?"), str(ins)[:300])
                ctr[0] += 1
                if n is None:
                    ins.then_inc(sems[k], inc)
                elif n in rank[k]:
                    ins.then_inc(sems[k], 1)
            if tail:
                for s_, v in final:
                    eng.wait_ge(s_, v)

        with nc.Block() as block:
            @block.sync
            def _(e):
                replay("sp", e, tail=True)

            @block.tensor
            def _(e):
                replay("pe", e)

            @block.scalar
            def _(e):
                replay("act", e)

            @block.vector
            def _(e):
                replay("dve", e)

            @block.gpsimd
            def _(e):
                replay("pool", e)
        self.es.close()


def build(TH, dbg=False):
    nc = bass.Bass("TRN2", target_bir_lowering=False)
    P = Prog(nc)
    NB = TH // TB
    TT = 2 * TH

    def ext(name, shape, dt=F32, kind="ExternalInput"):
        return P.dram(name, shape, dt, kind=kind)

    xp = ext("xp", [TT + 4, D])
    w_in = ext("w_in", [D, NCOL])
    lbl = ext("lbl", [4, 1024])
    conv5 = ext("conv5", [3072, 5])
    alog = ext("alog", [16])
    dtb = ext("dtb", [16])
    hgnw = ext("hgnw", [128])
    gdnw = ext("gdnw", [128])
    n1w = ext("n1w", [D])
    n2w = ext("n2w", [D])
    fnw = ext("fnw", [D])
    w_out = ext("w_out", [D, D])
    w1 = ext("w1", [D, DFF])
    w2 = ext("w2", [DFF, D])
    consts = ext("consts", [128, NCONST])
    out = ext("out", [TH, D], kind="ExternalOutput")

    hgw = {d: [P.dram(f"hgw{d}{h}", [128, 16, 512], BF16) for h in range(8)] for d in "AB"}
    gdw = [P.dram(f"gdw{h}", [128, 16, 512], BF16) for h in range(8)]
    bgw = {d: P.dram(f"bgw{d}", [128, 16, 16], BF16) for d in "AB"}
    wog = [P.dram(f"wog{i}", [128, 16, 256], BF16) for i in range(8)]
    w1g = [P.dram(f"w1g{i}", [128, 16, 256], BF16) for i in range(32)]
    w2g = [[P.dram(f"w2g{cb}_{fg}", [128, 8, 512], BF16) for fg in range(8)] for cb in range(4)]
    ost = P.dram("ost", [TH, D], F32)
    mixs = P.dram("mixs", [TH, D], BF16)

    win_v = w_in.re("(kt p) c -> p kt c", p=128)

    def conv_cols(dst, j, c0, n=128):
        P.dma("pool", dst[:, :, j * 128:j * 128 + n], win_v[:, :, c0:c0 + n])

    for h in range(8):
        for d, fo in (("A", 1024), ("B", 2048)):
            conv_cols(hgw[d][h], 0, 0 + h * 128)
            conv_cols(hgw[d][h], 1, fo + h * 128)
            conv_cols(hgw[d][h], 2, 3072 + h * 128)
            conv_cols(hgw[d][h], 3, 4096 + h * 128)
    for h in range(8):
        for j, base in enumerate((5120, 6144, 7168, 8192)):
            conv_cols(gdw[h], j, base + h * 128)
    for d, o in (("A", 0), ("B", 8)):
        P.dma("pool", bgw[d][:, :, 0:8], win_v[:, :, 9216 + o:9216 + o + 8])
        P.dma("pool", bgw[d][:, :, 8:16], win_v[:, :, 9232 + o:9232 + o + 8])
    late_conv = []
    wo_v = w_out.re("(kt p) c -> p kt c", p=128)
    for cb in range(8):
        late_conv.append((wog[cb], wo_v[:, :, cb * 256:(cb + 1) * 256]))
    w1_v = w1.re("(kt p) f -> p kt f", p=128)
    for fg in range(32):
        late_conv.append((w1g[fg], w1_v[:, :, fg * 256:(fg + 1) * 256]))
    w2_v = w2.re("(ft p) c -> p ft c", p=128)
    for cb in range(4):
        for fg in range(8):
            late_conv.append((w2g[cb][fg], w2_v[:, fg * 8:(fg + 1) * 8, cb * 512:(cb + 1) * 512]))

    psb = []
    for i in range(8):
        t = P.es.enter_context(nc.psum_tensor(f"ps{i}", [128, 512], F32))
        psb.append(t)
    bigs = Rot([V(psb[i][:], [Buf(True)]) for i in range(2)])
    qbufs = [Buf(True) for _ in range(5)]
    quarters = Rot([V(psb[2 + i % 5][:, (i // 5) * 128:(i // 5 + 1) * 128], [qbufs[i % 5]]) for i in range(20)])
    ops_slot = V(psb[7][:, 0:128], [Buf(True)])

    C = P.sb("C", [128, NCONST], F32)
    P.dma("sp", C, consts)
    ident = C[:, C_ID:C_ID + 128]
    ones = C[:, C_ONES:C_ONES + 128]
    identb = P.sb("identb", [128, 128], BF16)
    P.cp("dve", identb, ident)
    c128 = P.sb("c128", [128, 128], F32)
    P.ts("dve", c128, ones, 128.0, None, ALU.mult)

    def CD(d, off):
        b = C_DIR[d] + off
        return C[:, b:b + 128]

    P.arena_init(43600)
    n1w_b = P.al([128, D], F32)
    P.dma("sp", n1w_b, V(n1w.ap.partition_broadcast(128), n1w.bufs))
    oml = P.al([128, 2, 1024], F32)
    xts_l = P.sbn("xt", [128, D], F32, 2)
    for di in range(2):
        lt = xts_l[di]
        for r in range(2):
            P.dma("sp", lt[:, r * 1024:(r + 1) * 1024],
                  V(lbl.ap[2 * di + r:2 * di + r + 1, :].to_broadcast([128, 1024]), lbl.bufs))
        P.tt("dve", lt[:, 0:1024], lt[:, 0:1024], lt[:, 1024:2048], ALU.subtract)
        P.act(lt[:, 1024:2048], lt[:, 0:1024], AF.Exp)
        P.ts("dve", oml[:, di, :], lt[:, 1024:2048], 1.0, -1.0, ALU.add, ALU.pow)
    cw = P.al([128, 24, 5], F32)
    P.dma("sp", cw, conv5.re("(g p) j -> p g j", p=128))
    dtb_b = P.al([128, 16], F32)
    P.dma("sp", dtb_b, V(dtb.ap.partition_broadcast(128), dtb.bufs))
    nA_b = P.al([128, 16], F32)
    P.dma("sp", nA_b, V(alog.ap.partition_broadcast(128), alog.bufs))
    P.act(nA_b, nA_b, AF.Exp)
    P.ts("dve", nA_b, nA_b, -1.0, None, ALU.mult)
    hgnw_b = P.al([128, 128], F32)
    P.dma("sp", hgnw_b, V(hgnw.ap.partition_broadcast(128), hgnw.bufs))
    gdnw_b = P.al([128, 128], F32)
    P.dma("sp", gdnw_b, V(gdnw.ap.partition_broadcast(128), gdnw.bufs))

    dbg_out = {}

    def dbg_dump(name, v, shape, dt=F32):
        if not dbg:
            return
        if name in dbg_out:
            return
        t = P.dram("dbg_" + name, shape, dt, kind="ExternalOutput")
        dbg_out[name] = t
        P.dma("sp", t, v)

    xts = Rot(xts_l)
    junk = P.sb("junk", [128, D], BF16)
    hbs = Rot(P.sbn("hb", [128, D], BF16, 2))
    stat = Rot(P.sbn("stat", [128, 4], F32, 4))

    def norm_rows(xt, np_, wb):
        st = stat.next()
        hb = hbs.next()
        P.act(junk[0:np_, :], xt[0:np_, :], AF.Square, accum=st[0:np_, 0:1])
        P.ts("dve", st[0:np_, 1:2], st[0:np_, 0:1], 1.0 / D, EPS, ALU.mult, ALU.add)
        P.ts("dve", st[0:np_, 2:3], st[0:np_, 1:2], -0.5, None, ALU.pow)
        P.stt("dve", hb[0:np_, :], xt[0:np_, :], st[0:np_, 2:3], wb[0:np_, :], ALU.mult, ALU.mult)
        return hb, st

    def transpose_rows(hb, np_, dstT, c0):
        for g in range(4):
            ps = quarters.next()
            psv = ps.bitcast(BF16)
            if np_ > 64:
                ps = bigs.next()
                psv = ps.bitcast(BF16)[:, 0:512]
            pv = psv[:, 0:4 * np_].re("p (a b) -> p a b", a=4)
            for j in range(4):
                kt = g * 4 + j
                P.tr(pv[:, j, :], hb[0:np_, kt * 128:(kt + 1) * 128], identb[0:np_, 0:np_])
            P.cp("act" if g % 2 == 0 else "dve", dstT[:, g * 4:(g + 1) * 4, c0:c0 + np_], pv)

    hTs = Rot(P.aln([128, 16, TB + 4], BF16, 1))

    def make_hT(s0):
        hT = hTs.next()
        for i in range(5):
            np_ = 128 if i < 4 else 4
            xt = xts.next()
            P.dma("sp", xt[0:np_, :], xp[s0 + 128 * i:s0 + 128 * i + np_, :])
            hb, _ = norm_rows(xt, np_, n1w_b)
            transpose_rows(hb, np_, hT, 128 * i)
        return hT

    Sf = P.aln([128, 128], F32, 16)
    Sb = P.aln([128, 128], BF16, 16)
    wbufs = Rot(P.aln([128, 16, 512], BF16, 2))
    wbg_sb = {d: P.al([128, 16, 16], BF16) for d in "AB"}
    for d in "AB":
        P.dma("sp", wbg_sb[d], bgw[d])

    plan = [("A", b * TB) for b in range(NB)] + [("F", b * TB) for b in range(2 * NB - 1, NB - 1, -1)] + \
           [("B", b * TB) for b in range(NB - 1, -1, -1)]
    wseq = []
    for mode_, _s in plan:
        d_ = "A" if mode_ == "A" else "B"
        wseq += [hgw[d_][h] for h in range(8)] + [gdw[h] for h in range(8)]
    wstate = {"k": 0, "bufs": [wbufs.next(), wbufs.next()]}

    def wnext():
        k = wstate["k"]
        if late_conv and k >= 4:
            dst_, src_ = late_conv.pop(0)
            P.dma("pool", dst_, src_)
        if k == 0:
            P.dma("sp", wstate["bufs"][0], wseq[0])
        if k + 1 < len(wseq):
            P.dma("sp", wstate["bufs"][(k + 1) % 2], wseq[k + 1])
        wstate["k"] = k + 1
        return wstate["bufs"][k % 2]

    def zero_states():
        for h in range(16):
            P.memset("pool", Sf[h], 0.0)
            P.memset("pool", Sb[h], 0.0)

    w512 = Rot(P.aln([128, 512], F32, 10))
    w516 = Rot(P.aln([128, 516], F32, 3))
    keep512 = Rot(P.aln([128, 512], F32, 6))
    w128 = Rot(P.aln([128, 128], F32, 28))
    tset = [{"f": P.aln([128, 128], F32, 13), "b": P.aln([128, 128], BF16, 6)} for _ in range(4)]
    b128 = Rot(P.aln([128, 128], BF16, 8))
    zg = P.aln([128, 128], F32, 4)
    sm = Rot(P.aln([128, 16], F32, 24))
    bsm = Rot(P.aln([128, 16], F32, 20))

    def silu_from(dst, src_v, scale=1.0):
        e = w512.next()[:, 0:src_v.ap.shape[1]]
        P.act(e, src_v, AF.Exp, scale=-1.0)
        P.ts("dve", e, e, 1.0, -1.0, ALU.add, ALU.pow)
        P.stt("dve", dst, src_v, scale, e, ALU.mult, ALU.mult)

    def finalize_head(h, r0, o_ps_or_sb, gate_logits, nw_b):
        oa = w128.next()
        P.dma("sp", oa, ost[r0:r0 + 128, h * 128:(h + 1) * 128])
        ot = w128.next()
        P.tt("dve", ot, oa, o_ps_or_sb, ALU.add)
        st = sm.next()
        jk = w128.next()
        P.act(jk, ot, AF.Square, accum=st[:, 0:1])
        P.ts("dve", st[:, 1:2], st[:, 0:1], 1.0 / 128, EPS, ALU.mult, ALU.add)
        P.ts("dve", st[:, 2:3], st[:, 1:2], -0.5, None, ALU.pow)
        on = w128.next()
        P.stt("dve", on, ot, st[:, 2:3], nw_b, ALU.mult, ALU.mult)
        gs = w128.next()
        silu_from(gs, gate_logits)
        mx = b128.next()
        P.tt("pool", mx, on, gs, ALU.mult)
        P.dma("pool", mixs[r0:r0 + 128, h * 128:(h + 1) * 128], mx)

    def mixer_block(mode, s0):
        d = "A" if mode == "A" else "B"
        di = 0 if d == "A" else 1
        full = mode != "F"
        asc = d == "A"
        hT = make_hT(s0)
        tiles = list(range(4)) if asc else list(range(3, -1, -1))
        U2, M1, M2, UD, VD = (CD(d, o) for o in (C_U2, C_M1, C_M2, C_UD, C_VD))
        md16 = C[:, C_MD16:C_MD16 + 128]

        for h in range(8):
            wb = wnext()
            S_f, S_b = Sf[h], Sb[h]
            if full:
                qps = bigs.next()
                for kt in range(16):
                    P.mm(qps, wb[:, kt, 0:128], hT[:, kt, 2:2 + TB], start=(kt == 0), stop=(kt == 15))
                qs = keep512.next()
                silu_from(qs, qps, scale=128.0 ** -0.5)
            ncol = 384 if mode == "B" else 256
            for tiles_g in [tiles[k:k + GRP_HG] for k in range(0, 4, GRP_HG)]:
                R = {}
                for i in tiles_g:
                    f, bb = tset[i]["f"], tset[i]["b"]
                    tps = bigs.next()
                    for kt in range(16):
                        P.mm(tps[:, 0:ncol], hT[:, kt, 2 + 128 * i:2 + 128 * i + 128], wb[:, kt, 128:128 + ncol],
                             start=(kt == 0), stop=(kt == 15))
                    P.act(f[0], tps[:, 0:128], AF.Exp, scale=-1.0)
                    P.cp("act", bb[0], tps[:, 128:256])
                    if mode == "B":
                        P.cp("act", f[8], tps[:, 256:384])
                for i in tiles_g:
                    f = tset[i]["f"]
                    P.ts("dve", f[1], f[0], 1.0, -1.0, ALU.add, ALU.pow)
                for i in tiles_g:
                    f = tset[i]["f"]
                    P.tt("dve", f[2], f[0], f[1], ALU.mult)
                    P.tt("pool", f[2], f[2], oml[:, di, h * 128:(h + 1) * 128], ALU.mult)
                for i in tiles_g:
                    f = tset[i]["f"]
                    P.act(f[3], f[2], AF.Ln, bias=1.0, scale=-1.0)
                for i in tiles_g:
                    f = tset[i]["f"]
                    r_ = {}
                    r_["b"] = quarters.next()
                    P.mm(r_["b"], f[3], U2)
                    r_["rem"] = quarters.next()
                    P.mm(r_["rem"], M2, f[3])
                    if full:
                        r_["bmr"] = quarters.next()
                        P.mm(r_["bmr"], f[3], M1)
                        r_["kT"] = quarters.next()
                        P.tr(r_["kT"], f[2], ident)
                    R[i] = r_
                for i in tiles_g:
                    f, bb = tset[i]["f"], tset[i]["b"]
                    P.act(f[4], R[i]["b"], AF.Exp)
                    P.act(f[5], R[i]["rem"], AF.Exp)
                    if full:
                        P.act(f[6], R[i]["bmr"], AF.Exp)
                        P.act(f[7], R[i]["bmr"], AF.Exp, scale=-1.0)
                for i in tiles_g:
                    f, bb = tset[i]["f"], tset[i]["b"]
                    P.tt("dve", bb[1], f[2], f[5], ALU.mult)
                    if full:
                        qsl = qs[:, 128 * i:128 * i + 128]
                        P.tt("pool", bb[2], qsl, f[6], ALU.mult)
                        P.tt("dve", bb[3], R[i]["kT"], f[7], ALU.mult)
                        P.tt("pool", bb[4], qsl, f[4], ALU.mult)
                if full:
                    for i in tiles_g:
                        bb = tset[i]["b"]
                        R[i]["AT"] = quarters.next()
                        P.mm(R[i]["AT"], bb[3], bb[2])
                    for i in tiles_g:
                        bb = tset[i]["b"]
                        P.tt("dve", bb[5], R[i]["AT"], U2, ALU.mult)
                for i in tiles_g:
                    f, bb = tset[i]["f"], tset[i]["b"]
                    v_bf, kst, qin, ATm, Ec = bb[0], bb[1], bb[4], bb[5], f[4]
                    if full:
                        o_ps = ops_slot
                        P.mm(o_ps, ATm, v_bf, start=True, stop=False)
                    chunks = (0, 1) if asc else (1, 0)
                    for ci, c in enumerate(chunks):
                        rs = slice(64 * c, 64 * c + 64)
                        if full:
                            P.mm(o_ps[rs, :], qin[:, rs], S_b, start=False, stop=True)
                        kv_ps = quarters.next()
                        P.mm(kv_ps, kst[rs, :], v_bf[rs, :])
                        dcol = 64 * c + 63 if asc else 64 * c
                        P.stt("dve", S_f, S_f, Ec[:, dcol:dcol + 1], kv_ps, ALU.mult, ALU.add)
                        P.cp("act", S_b, S_f)
                    if full:
                        r0 = s0 + 128 * i
                        if mode == "A":
                            osb = w128.next()
                            P.cp("act", osb, o_ps)
                            P.dma("pool", ost[r0:r0 + 128, h * 128:(h + 1) * 128], osb)
                        else:
                            finalize_head(h, r0, o_ps, f[8], hgnw_b)

        bet, gg, egc, dch, bega = {}, {}, {}, {}, {}
        for i in range(4):
            bps = quarters.next()
            for kt in range(16):
                P.mm(bps[:, 0:16], hT[:, kt, 2 + 128 * i:2 + 128 * i + 128], wbg_sb[d][:, kt, :],
                     start=(kt == 0), stop=(kt == 15))
            t1 = sm.next()
            P.act(t1[:, 0:8], bps[:, 0:8], AF.Exp, scale=-1.0)
            be = bsm.next()
            P.ts("dve", be[:, 0:8], t1[:, 0:8], 1.0, -1.0, ALU.add, ALU.pow)
            z = sm.next()
            P.tt("dve", z[:, 0:8], bps[:, 8:16], dtb_b[:, 8 * di:8 * di + 8], ALU.add)
            P.act(z[:, 0:8], z[:, 0:8], AF.Exp)
            P.act(z[:, 8:16], z[:, 0:8], AF.Ln, bias=1.0)
            g_ = bsm.next()
            P.tt("dve", g_[:, 0:8], z[:, 8:16], nA_b[:, 8 * di:8 * di + 8], ALU.mult)
            cps = quarters.next()
            P.mm(cps[:, 0:8], UD, g_[:, 0:8])
            P.mm(cps[:, 8:16], VD, g_[:, 0:8])
            P.mm(cps[:, 16:24], ones, g_[:, 0:8])
            ex = bsm.next()
            P.act(ex[:, 0:8], cps[:, 0:8], AF.Exp)
            P.act(ex[:, 8:16], cps[:, 8:16], AF.Exp)
            dc = bsm.next()
            P.act(dc[:, 0:8], cps[:, 16:24], AF.Exp)
            bg_ = bsm.next()
            P.tt("dve", bg_[:, 0:8], be[:, 0:8], ex[:, 0:8], ALU.mult)
            P.ts("dve", bg_[:, 8:16], be[:, 0:8], -1.0, None, ALU.mult)
            bet[i], gg[i], egc[i], dch[i], bega[i] = be, g_, ex, dc, bg_

        def gdn_proj():
            wb_ = wnext()
            us_ = {}
            jj_ = {"q": 0, "k": 1, "v": 2}
            for nm in [nm for nm in "qkv" if not (nm == "q" and not full)]:
                j = jj_[nm]
                pa = bigs.next()
                pb = quarters.next()
                for kt in range(16):
                    P.mm(pa, wb_[:, kt, j * 128:(j + 1) * 128], hT[:, kt, 0:TB], start=(kt == 0), stop=(kt == 15))
                for kt in range(16):
                    P.mm(pb[:, 0:4], wb_[:, kt, j * 128:(j + 1) * 128], hT[:, kt, TB:TB + 4],
                         start=(kt == 0), stop=(kt == 15))
                u = w516.next()
                P.cp("act", u[:, 0:TB], pa)
                P.cp("act", u[:, TB:TB + 4], pb[:, 0:4])
                us_[nm] = u
            return wb_, us_

        pre_gdn = {"v": gdn_proj()}
        for h in range(8):
            wb, us = pre_gdn["v"]
            S_f, S_b = Sf[8 + h], Sb[8 + h]
            outs = {}
            names = [nm for nm in "qkv" if not (nm == "q" and not full)]
            jj = {"q": 0, "k": 1, "v": 2}
            ysd, yd = {}, {}
            for nm in names:
                y = w512.next()
                g24 = jj[nm] * 8 + h
                u = us[nm]
                P.ts("dve", y, u[:, 0:TB], cw[:, g24, 0:1], None, ALU.mult)
                for tp in range(1, 5):
                    P.stt("dve", y, u[:, tp:tp + TB], cw[:, g24, tp:tp + 1], y, ALU.mult, ALU.add)
                yd[nm] = y
            for nm in names:
                ys = keep512.next() if nm == "v" else w512.next()
                silu_from(ys, yd[nm])
                ysd[nm] = ys
            outs["v"] = ysd["v"]
            sqs, spss = {}, {}
            for nm in names:
                if nm == "v":
                    continue
                sqs[nm] = w512.next()
                P.act(sqs[nm], ysd[nm], AF.Square)
            for nm in names:
                if nm == "v":
                    continue
                spss[nm] = bigs.next()
                P.mm(spss[nm], c128 if nm == "q" else ones, sqs[nm])
            for nm in names:
                if nm == "v":
                    continue
                r = w512.next()
                P.ts("dve", r, spss[nm], (128.0 if nm == "q" else 1.0) * EPS, -0.5, ALU.add, ALU.pow)
                nrm = keep512.next()
                P.tt("pool", nrm, ysd[nm], r, ALU.mult)
                outs[nm] = nrm
            knT, vT = outs["k"], outs["v"]
            qnT = outs.get("q")
            for tiles_g in [tiles[k:k + GRP_GDN] for k in range(0, 4, GRP_GDN)]:
                R = {i: {} for i in tiles_g}
                for i in tiles_g:
                    cs = slice(128 * i, 128 * i + 128)
                    R[i]["ktm"] = quarters.next()
                    P.tr(R[i]["ktm"], knT[:, cs], ident)
                    R[i]["vtm"] = quarters.next()
                    P.tr(R[i]["vtm"], vT[:, cs], ident)
                for i in tiles_g:
                    f, bb = tset[i]["f"], tset[i]["b"]
                    P.ts("dve", f[0], R[i]["ktm"], bega[i][:, h:h + 1], None, ALU.mult)
                    P.ts("dve", f[1], R[i]["vtm"], bet[i][:, h:h + 1], None, ALU.mult)
                    P.ts("dve", bb[0], R[i]["ktm"], egc[i][:, 8 + h:9 + h], None, ALU.mult)
                    R[i]["GU"] = w128.next()
                    P.ts("dve", R[i]["GU"], UD, gg[i][:, h:h + 1], None, ALU.mult)
                    if full:
                        R[i]["GV"] = w128.next()
                        P.ts("dve", R[i]["GV"], VD, gg[i][:, h:h + 1], None, ALU.mult)
                for i in tiles_g:
                    cs = slice(128 * i, 128 * i + 128)
                    R[i]["D1"] = quarters.next()
                    P.mm(R[i]["D1"], R[i]["GU"], VD)
                    R[i]["G"] = quarters.next()
                    P.mm(R[i]["G"], knT[:, cs], knT[:, cs])
                for i in tiles_g:
                    R[i]["E1"] = w128.next()
                    P.act(R[i]["E1"], R[i]["D1"], AF.Exp)
                for i in tiles_g:
                    f = tset[i]["f"]
                    P.stt("dve", f[2], R[i]["G"], bega[i][:, 8 + h:9 + h], R[i]["E1"], ALU.mult, ALU.mult)
                    P.tt("pool", f[2], f[2], VD, ALU.mult)
                if full:
                    for i in tiles_g:
                        cs = slice(128 * i, 128 * i + 128)
                        R[i]["D2"] = quarters.next()
                        P.mm(R[i]["D2"], R[i]["GV"], UD)
                        R[i]["KQ"] = quarters.next()
                        P.mm(R[i]["KQ"], knT[:, cs], qnT[:, cs])
                    for i in tiles_g:
                        R[i]["E2"] = w128.next()
                        P.act(R[i]["E2"], R[i]["D2"], AF.Exp)
                    for i in tiles_g:
                        cs = slice(128 * i, 128 * i + 128)
                        bb = tset[i]["b"]
                        qk0 = w128.next()
                        P.tt("dve", qk0, R[i]["KQ"], R[i]["E2"], ALU.mult)
                        P.tt("pool", bb[1], qk0, UD, ALU.mult)
                        P.cp("pool", bb[2], qnT[:, cs])
                for i in tiles_g:
                    R[i]["XTp"] = quarters.next()
                    P.tr(R[i]["XTp"], tset[i]["f"][2], ident)
                for i in tiles_g:
                    f = tset[i]["f"]
                    P.cp("act", f[3], R[i]["XTp"])
                for i in tiles_g:
                    f = tset[i]["f"]
                    P.tt("pool", f[4], f[2], md16, ALU.mult)
                    P.tt("pool", f[6], f[3], md16, ALU.mult)
                    P.tt("dve", f[8], f[4], ident, ALU.add)
                    P.tt("dve", f[10], f[6], ident, ALU.add)
                cur = 0
                for lev in range(3):
                    nx = 1 - cur
                    for i in tiles_g:
                        f = tset[i]["f"]
                        R[i]["y2p"] = quarters.next()
                        P.mm(R[i]["y2p"], f[6 + cur], f[4 + cur])
                        R[i]["yt2p"] = quarters.next()
                        P.mm(R[i]["yt2p"], f[4 + cur], f[6 + cur])
                    for i in tiles_g:
                        f = tset[i]["f"]
                        P.cp("act", f[4 + nx], R[i]["y2p"])
                        P.cp("act", f[6 + nx], R[i]["yt2p"])
                    for i in tiles_g:
                        f = tset[i]["f"]
                        R[i]["tp"] = quarters.next()
                        P.mm(R[i]["tp"], f[6 + nx], f[8 + cur])
                        R[i]["ttp"] = quarters.next()
                        P.mm(R[i]["ttp"], f[4 + nx], f[10 + cur])
                    for i in tiles_g:
                        f = tset[i]["f"]
                        P.tt("dve", f[8 + nx], R[i]["tp"], f[8 + cur], ALU.add)
                        P.tt("dve", f[10 + nx], R[i]["ttp"], f[10 + cur], ALU.add)
                    cur = nx
                for B_ in (32, 64, 128):
                    nx = 1 - cur
                    mo = C[:, C_MOFF[B_]:C_MOFF[B_] + 128]
                    for i in tiles_g:
                        f = tset[i]["f"]
                        R[i]["N"] = w128.next()
                        P.tt("pool", R[i]["N"], f[2], mo, ALU.mult)
                        R[i]["NT"] = w128.next()
                        P.tt("pool", R[i]["NT"], f[3], mo, ALU.mult)
                    for i in tiles_g:
                        f = tset[i]["f"]
                        R[i]["ap"] = quarters.next()
                        P.mm(R[i]["ap"], R[i]["N"], f[10 + cur])
                        if B_ < 128:
                            R[i]["bp"] = quarters.next()
                            P.mm(R[i]["bp"], R[i]["NT"], f[8 + cur])
                    for i in tiles_g:
                        R[i]["A"] = w128.next()
                        P.cp("act", R[i]["A"], R[i]["ap"])
                        if B_ < 128:
                            R[i]["B"] = w128.next()
                            P.cp("act", R[i]["B"], R[i]["bp"])
                    for i in tiles_g:
                        f = tset[i]["f"]
                        R[i]["ttp"] = quarters.next()
                        P.mm(R[i]["ttp"], f[8 + cur], R[i]["A"])
                        if B_ < 128:
                            R[i]["tp"] = quarters.next()
                            P.mm(R[i]["tp"], f[10 + cur], R[i]["B"])
                    for i in tiles_g:
                        f = tset[i]["f"]
                        P.tt("dve", f[10 + nx], R[i]["ttp"], f[10 + cur], ALU.add)
                        if B_ < 128:
                            P.tt("dve", f[8 + nx], R[i]["tp"], f[8 + cur], ALU.add)
                    cur = nx
                for i in tiles_g:
                    f = tset[i]["f"]
                    R[i]["wT"] = quarters.next()
                    P.mm(R[i]["wT"], f[0], f[10 + cur])
                    R[i]["u"] = quarters.next()
                    P.mm(R[i]["u"], f[10 + cur], f[1])
                for i in tiles_g:
                    f, bb = tset[i]["f"], tset[i]["b"]
                    P.cp("act", bb[3], R[i]["wT"])
                    P.cp("act", f[12], R[i]["u"])
                if mode == "B":
                    for i in tiles_g:
                        zps = bigs.next()
                        for kt in range(16):
                            P.mm(zps[:, 0:128], hT[:, kt, 2 + 128 * i:2 + 128 * i + 128], wb[:, kt, 384:512],
                                 start=(kt == 0), stop=(kt == 15))
                        P.cp("act", zg[i], zps[:, 0:128])
                if h + 1 < 8 and tiles_g[-1] == tiles[-1]:
                    pre_gdn["v"] = gdn_proj()
                for i in tiles_g:
                    f, bb = tset[i]["f"], tset[i]["b"]
                    kst, qkT, qn_bf, wT_bf, vnew, u_sb = bb[0], bb[1], bb[2], bb[3], bb[4], f[12]
                    ws_ps = quarters.next()
                    P.mm(ws_ps, wT_bf, S_b)
                    P.tt("dve", vnew, u_sb, ws_ps, ALU.subtract)
                    if full:
                        o1 = quarters.next()
                        P.mm(o1, qn_bf, S_b)
                        o2 = quarters.next()
                        P.mm(o2, qkT, vnew)
                    kv_ps = quarters.next()
                    P.mm(kv_ps, kst, vnew)
                    P.stt("dve", S_f, S_f, dch[i][:, h:h + 1], kv_ps, ALU.mult, ALU.add)
                    P.cp("act", S_b, S_f)
                    if full:
                        o1s = w128.next()
                        P.ts("dve", o1s, o1, egc[i][:, h:h + 1], None, ALU.mult)
                        osb = w128.next()
                        P.tt("dve", osb, o1s, o2, ALU.add)
                        r0 = s0 + 128 * i
                        hh = 8 + h
                        if mode == "A":
                            P.dma("pool", ost[r0:r0 + 128, hh * 128:(hh + 1) * 128], osb)
                        else:
                            finalize_head(hh, r0, osb, zg[i], gdnw_b)

    for pi, (mode_, s0_) in enumerate(plan):
        if pi == 0 or pi == NB:
            zero_states()
        mixer_block(mode_, s0_)

    while late_conv:
        dst_, src_ = late_conv.pop(0)
        P.dma("pool", dst_, src_)
    P.barrier()
    P.aoff = 0
    bigs.items = [V(psb[i][:], [Buf(True)]) for i in range(8)]
    bigs.i = 0
    n2w_b = P.al([128, D], F32)
    P.dma("sp", n2w_b, V(n2w.ap.partition_broadcast(128), n2w.bufs))
    fnw_b = P.al([128, D], F32)
    P.dma("sp", fnw_b, V(fnw.ap.partition_broadcast(128), fnw.bufs))
    mixT = P.al([128, 16, TB], BF16)
    h2T = mixT
    hidT = P.al([128, 64, TB], BF16)
    x1 = P.aln([128, D], F32, 4)
    mts = hbs
    wo_sb = Rot(P.aln([128, 16, 256], BF16, 2))
    w2_sb = Rot(P.aln([128, 8, 512], BF16, 2))
    rl = Rot(P.aln([128, 512], F32, 3))

    for b in range(NB):
        s0 = b * TB
        for i in range(4):
            mt = mts.next()
            P.dma("sp", mt, mixs[s0 + 128 * i:s0 + 128 * i + 128, :])
            transpose_rows(mt, 128, mixT, 128 * i)
        for i in range(4):
            P.dma("sp", x1[i], xp[2 + s0 + 128 * i:2 + s0 + 128 * i + 128, :])
        for cb in range(8):
            wo = wo_sb.next()
            P.dma("sp", wo, wog[cb])
            for i in range(4):
                ps = bigs.next()
                for kt in range(16):
                    P.mm(ps[:, 0:256], mixT[:, kt, 128 * i:128 * i + 128], wo[:, kt, :], start=(kt == 0), stop=(kt == 15))
                P.tt("dve", x1[i][:, cb * 256:(cb + 1) * 256], x1[i][:, cb * 256:(cb + 1) * 256], ps[:, 0:256], ALU.add)
        for i in range(4):
            hb, _ = norm_rows(x1[i], 128, n2w_b)
            transpose_rows(hb, 128, h2T, 128 * i)
        for fg in range(32):
            w1s = wo_sb.next()
            P.dma("sp", w1s, w1g[fg])
            for f in range(2):
                ps = bigs.next()
                for kt in range(16):
                    P.mm(ps, w1s[:, kt, f * 128:(f + 1) * 128], h2T[:, kt, :], start=(kt == 0), stop=(kt == 15))
                r = rl.next()
                P.act(r, ps, AF.Relu)
                P.tt("dve" if f % 2 else "pool", hidT[:, fg * 2 + f, :], r, r, ALU.mult)
        for cb in range(4):
            accs = [bigs.next() for _ in range(4)]
            for fg in range(8):
                w2s = w2_sb.next()
                P.dma("sp", w2s, w2g[cb][fg])
                for i in range(4):
                    for f in range(8):
                        ft = fg * 8 + f
                        P.mm(accs[i], hidT[:, ft, 128 * i:128 * i + 128], w2s[:, f, :], start=(ft == 0), stop=(ft == 63))
            for i in range(4):
                P.tt("dve", x1[i][:, cb * 512:(cb + 1) * 512], x1[i][:, cb * 512:(cb + 1) * 512], accs[i], ALU.add)
        for i in range(4):
            st = stat.next()
            xo = xts.next()
            P.act(junk, x1[i], AF.Square, accum=st[:, 0:1])
            P.ts("dve", st[:, 1:2], st[:, 0:1], 1.0 / D, EPS, ALU.mult, ALU.add)
            P.ts("dve", st[:, 2:3], st[:, 1:2], -0.5, None, ALU.pow)
            P.stt("dve", xo, x1[i], st[:, 2:3], fnw_b, ALU.mult, ALU.mult)
            P.dma("pool", out[s0 + 128 * i:s0 + 128 * i + 128, :], xo)

    P.emit()
    return nc, list(dbg_out.keys())


_NC_CACHE = {}


def _prep_core_inputs(inp, core, TH):
    b, half = core // 2, core % 2
    T = 2 * TH
    x = np.asarray(inp["x"], np.float32)[b, :T]
    w_in = np.asarray(inp["w_in"], np.float32)[0]
    lbf = np.asarray(inp["hg_lb_logits_fwd"], np.float32)
    lbb = np.asarray(inp["hg_lb_logits_bwd"], np.float32)
    cw = np.asarray(inp["gdn_conv_w"], np.float32)[0]
    alog = np.asarray(inp["gdn_a_log"], np.float32)[0]
    dtb = np.asarray(inp["gdn_dt_bias"], np.float32)[0]
    z1 = np.zeros((1, 3072), np.float32)
    if half == 0:
        xs = x
        wi = w_in
        lbl = np.concatenate([lbf, lbb], 0)
        c5 = np.concatenate([cw, z1], 0)
        al, dt = alog, dtb
    else:
        xs = x[::-1]
        wi = w_in.copy()
        wi[:, 1024:2048] = w_in[:, 2048:3072]
        wi[:, 2048:3072] = w_in[:, 1024:2048]
        wi[:, 9216:9224] = w_in[:, 9224:9232]
        wi[:, 9224:9232] = w_in[:, 9216:9224]
        wi[:, 9232:9240] = w_in[:, 9240:9248]
        wi[:, 9240:9248] = w_in[:, 9232:9240]
        lbl = np.concatenate([lbb, lbf], 0)
        c5 = np.concatenate([z1, cw[::-1]], 0)
        al, dt = alog[::-1], dtb[::-1]
    xpad = np.zeros((T + 4, D), np.float32)
    xpad[2:T + 2] = xs
    return {
        "xp": xpad,
        "w_in": np.ascontiguousarray(wi),
        "lbl": np.ascontiguousarray(lbl),
        "conv5": np.ascontiguousarray(c5.T),
        "alog": np.ascontiguousarray(al).reshape(16),
        "dtb": np.ascontiguousarray(dt).reshape(16),
        "hgnw": np.asarray(inp["hg_norm_w"], np.float32)[0],
        "gdnw": np.asarray(inp["gdn_norm_w"], np.float32)[0],
        "n1w": np.asarray(inp["norm1_w"], np.float32)[0],
        "n2w": np.asarray(inp["norm2_w"], np.float32)[0],
        "fnw": np.asarray(inp["final_norm_w"], np.float32),
        "w_out": np.asarray(inp["w_out"], np.float32)[0],
        "w1": np.asarray(inp["w_mlp_in"], np.float32)[0],
        "w2": np.asarray(inp["w_mlp_out"], np.float32)[0],
        "consts": make_consts(),
    }


def run(inputs, TH, dbg=False):
    key = (TH, dbg)
    if key not in _NC_CACHE:
        _NC_CACHE[key] = build(TH, dbg)
    nc, dbg_names = _NC_CACHE[key]
    in_maps = [_prep_core_inputs(inputs, c, TH) for c in range(8)]
    res = run_bass_kernel_spmd(nc, in_maps, core_ids=list(range(8)))
    B = 4
    outp = np.zeros((B, 2 * TH, D), np.float32)
    for c in range(8):
        b, half = c // 2, c % 2
        o = np.asarray(res.results[c]["out"], np.float32)
        if half == 0:
            outp[b, :TH] = o
        else:
            outp[b, TH:] = o[::-1]
    return outp, res


def kernel(**inputs):
    x = np.asarray(inputs["x"])
    TH = x.shape[1] // 2
    outp, _ = run(inputs, TH)
    return outp
```

```python
import numpy as np
from contextlib import ExitStack
import concourse.bass as bass
import concourse.mybir as mybir
from concourse.bass_utils import run_bass_kernel_spmd

F32 = mybir.dt.float32
BF16 = mybir.dt.bfloat16
ALU = mybir.AluOpType
AF = mybir.ActivationFunctionType
AX = mybir.AxisListType

D = 2048
NCOL = 9248
DFF = 8192
EPS = 1e-6
TB = 512

C_ID, C_ONES = 0, 128
C_U2, C_M1, C_M2, C_UD, C_VD = 0, 128, 256, 384, 512
C_DIR = {"A": 256, "B": 256 + 640}
C_MD16 = 256 + 1280
C_MOFF = {32: C_MD16 + 128, 64: C_MD16 + 256, 128: C_MD16 + 384}
NCONST = 256 + 1280 + 512


def make_consts():
    c = np.zeros((128, NCONST), np.float32)
    c[:, C_ID:C_ID + 128] = np.eye(128, dtype=np.float32)
    c[:, C_ONES:C_ONES + 128] = 1.0
    s = np.arange(128)[:, None]
    t = np.arange(128)[None, :]
    same = (s // 64) == (t // 64)
    c[:, C_MD16:C_MD16 + 128] = (s // 16) == (t // 16)
    for B_ in (32, 64, 128):
        c[:, C_MOFF[B_]:C_MOFF[B_] + 128] = ((s // B_) == (t // B_)) & ((s // (B_ // 2)) != (t // (B_ // 2)))
    for d in ("A", "B"):
        b = C_DIR[d]
        if d == "A":
            u2 = same & (s <= t)
            m1 = u2.astype(np.float32) - (same & ((s % 64) <= 31)).astype(np.float32)
            m2 = same & (s > t)
            ud = s <= t
            vd = s > t
        else:
            u2 = same & (s >= t)
            m1 = u2.astype(np.float32) - (same & ((s % 64) >= 32)).astype(np.float32)
            m2 = same & (s < t)
            ud = s >= t
            vd = s < t
        c[:, b + C_U2:b + C_U2 + 128] = u2
        c[:, b + C_M1:b + C_M1 + 128] = m1
        c[:, b + C_M2:b + C_M2 + 128] = m2
        c[:, b + C_UD:b + C_UD + 128] = ud
        c[:, b + C_VD:b + C_VD + 128] = vd
    return c


class Buf:
    __slots__ = ("w", "r", "excl")

    def __init__(self, excl=False):
        self.w = {}
        self.r = {}
        self.excl = excl


class V:
    __slots__ = ("ap", "bufs")

    def __init__(self, ap, bufs):
        self.ap = ap
        self.bufs = tuple(bufs)

    def __getitem__(self, k):
        return V(self.ap[k], self.bufs)

    def bc(self, shape):
        return V(self.ap.to_broadcast(shape), self.bufs)

    def un(self, ax):
        return V(self.ap.unsqueeze(ax), self.bufs)

    def re(self, s, **kw):
        return V(self.ap.rearrange(s, **kw), self.bufs)

    def bitcast(self, dt):
        return V(self.ap.bitcast(dt), self.bufs)


class Rot:
    def __init__(self, items):
        self.items = items
        self.i = 0

    def next(self):
        v = self.items[self.i % len(self.items)]
        self.i += 1
        return v


ENGS = ("pe", "act", "dve", "pool", "sp")
DEBUG_LABELS = False
GRP_HG = 4
GRP_GDN = 4
LABELS = {}
NSLOT = {"sp": 12, "pool": 12}


class Prog:
    def __init__(self, nc):
        self.nc = nc
        self.lists = {e: [] for e in ENGS}
        self.cnt = {e: 0 for e in ENGS}
        self.seen = {e: {} for e in ENGS}
        self.slot_val = {}
        self.slot_i = {"sp": 0, "pool": 0}
        self.es = ExitStack()

    def sb(self, name, shape, dt, nbuf=1):
        t = self.es.enter_context(self.nc.sbuf_tensor(name, list(shape), dt))
        return V(t[:], [Buf() for _ in range(nbuf)])

    def sbn(self, name, shape, dt, n):
        return [self.sb(f"{name}{i}", shape, dt) for i in range(n)]

    def arena_init(self, cols):
        self.arena = self.sb("arena", [128, cols], F32)
        self.acols = cols
        self.aoff = 0

    def al(self, shape, dt):
        n = 1
        for d_ in shape[1:]:
            n *= d_
        nb = n * (4 if dt == F32 else 2)
        nb = (nb + 63) // 64 * 64
        cols = nb // 4
        assert self.aoff + cols <= self.acols, ("arena overflow", self.aoff, cols, self.acols)
        ap = self.arena.ap[:, self.aoff:self.aoff + cols]
        if dt != F32:
            ap = ap.bitcast(dt)
        ap = ap[:, 0:n]
        if len(shape) == 3:
            ap = ap.rearrange("p (a b) -> p a b", a=shape[1])
        self.aoff += cols
        return V(ap, [Buf()])

    def aln(self, shape, dt, n):
        return [self.al(shape, dt) for _ in range(n)]

    def barrier(self):
        for e in ENGS:
            waits = []
            sn = self.seen[e]
            for k in ENGS:
                if k in ("sp", e) or self.cnt[k] == 0:
                    continue
                if sn.get(k, 0) < self.cnt[k]:
                    sn[k] = self.cnt[k]
                    waits.append((k, self.cnt[k]))
            for slot, val in self.slot_val.items():
                if sn.get(slot, 0) < val:
                    sn[slot] = val
                    waits.append((slot, val))
            self.lists[e].append((waits, None, None, None))

    def dram(self, name, shape, dt, kind="Internal"):
        t = self.nc.dram_tensor(name, list(shape), dt, kind=kind)
        return V(t.ap(), [Buf()])

    def _deps(self, eng, reads, writes):
        need = {}

        def req(key, val, war=False):
            if key == eng and (eng == "pe" or war):
                return
            if need.get(key, 0) < val:
                need[key] = val

        for v in reads:
            for b in v.bufs:
                for k, val in b.w.items():
                    req(k, val)
                if b.excl:
                    for k, val in b.r.items():
                        req(k, val, war=True)
        for v in writes:
            for b in v.bufs:
                for k, val in b.w.items():
                    req(k, val)
                for k, val in b.r.items():
                    req(k, val, war=True)
        waits = []
        sn = self.seen[eng]
        for key, val in need.items():
            if sn.get(key, 0) >= val:
                continue
            sn[key] = val
            waits.append((key, val))
        return waits

    def _mark(self, key, val, reads, writes):
        for v in reads:
            for b in v.bufs:
                b.r[key] = val
        for v in writes:
            for b in v.bufs:
                b.w[key] = val
                b.r = {}

    def op(self, eng, fn, reads, writes):
        if DEBUG_LABELS:
            import sys as _s
            f = _s._getframe(2)
            f2 = f.f_back
            fn._lbl = f"{f.f_code.co_name}:{f.f_lineno}<{f2.f_code.co_name}:{f2.f_lineno}"
        waits = self._deps(eng, reads, writes)
        self.cnt[eng] += 1
        n = self.cnt[eng]
        self._mark(eng, n, reads, writes)
        self.lists[eng].append((waits, fn, (eng, 1), n))

    def dma(self, q, out, in_):
        reads, writes = [in_], [out]
        waits = self._deps(q, reads, writes)
        slot = (q, self.slot_i[q] % NSLOT[q])
        self.slot_i[q] += 1
        prev = self.slot_val.get(slot, 0)
        if prev > 0 and self.seen[q].get(slot, 0) < prev:
            self.seen[q][slot] = prev
            waits.append((slot, prev))
        new = prev + 16
        self.slot_val[slot] = new
        self._mark(slot, new, reads, writes)
        oa, ia = out.ap, in_.ap
        self.lists[q].append((waits, lambda e: e.dma_start(out=oa, in_=ia), (slot, 16), None))

    def mm(self, out, lhsT, rhs, start=True, stop=True):
        oa, la, ra = out.ap, lhsT.ap, rhs.ap
        self.op("pe", lambda e: e.matmul(oa, la, ra, start=start, stop=stop), [lhsT, rhs], [out])

    def tr(self, out, in_, ident):
        oa, ia, da = out.ap, in_.ap, ident.ap
        self.op("pe", lambda e: e.transpose(oa, ia, da), [in_, ident], [out])

    def act(self, out, in_, func, bias=None, scale=None, accum=None):
        oa, ia = out.ap, in_.ap
        kw = {}
        reads = [in_]
        writes = [out]
        if bias is not None:
            if isinstance(bias, V):
                kw["bias"] = bias.ap
                reads.append(bias)
            else:
                kw["bias"] = float(bias)
        if scale is not None:
            if isinstance(scale, V):
                kw["scale"] = scale.ap
                reads.append(scale)
            else:
                kw["scale"] = float(scale)
        if accum is not None:
            kw["accum_out"] = accum.ap
            writes.append(accum)
        self.op("act", lambda e: e.activation(oa, ia, func, **kw), reads, writes)

    def tt(self, eng, out, a, b, op):
        oa, aa, ba = out.ap, a.ap, b.ap
        self.op(eng, lambda e: e.tensor_tensor(oa, aa, ba, op), [a, b], [out])

    def ts(self, eng, out, a, s1, s2, op0, op1=None):
        oa, aa = out.ap, a.ap
        reads = [a]
        x1 = s1
        x2 = s2
        if isinstance(s1, V):
            x1 = s1.ap
            reads.append(s1)
        if isinstance(s2, V):
            x2 = s2.ap
            reads.append(s2)
        if op1 is None:
            if op0 == ALU.pow:
                self.act(out, a, AF.Ln)
                self.act(out, out, AF.Exp, scale=float(x1))
                return
            if eng == "pool":
                eng = "dve"
            self.op(eng, lambda e: e.tensor_scalar(oa, aa, x1, None, op0), reads, [out])
        elif op1 == ALU.pow:
            if eng == "pool":
                eng = "dve"
            self.op(eng, lambda e: e.tensor_scalar(oa, aa, x1, None, op0), reads, [out])
            if x2 == -1.0:
                self.op("dve", lambda e: e.reciprocal(oa, oa), [out], [out])
            else:
                self.act(out, out, AF.Ln)
                self.act(out, out, AF.Exp, scale=float(x2))
        else:
            self.op(eng, lambda e: e.tensor_scalar(oa, aa, x1, x2, op0, op1), reads, [out])

    def stt(self, eng, out, a, s, b, op0, op1):
        oa, aa, ba = out.ap, a.ap, b.ap
        reads = [a, b]
        x = s
        if isinstance(s, V):
            x = s.ap
            reads.append(s)
        self.op(eng, lambda e: e.scalar_tensor_tensor(oa, aa, x, ba, op0, op1), reads, [out])

    def amul(self, out, in_, sc):
        oa, ia, sa = out.ap, in_.ap, sc.ap
        self.op("act", lambda e: e.mul(oa, ia, sa), [in_, sc], [out])

    def cp(self, eng, out, in_):
        oa, ia = out.ap, in_.ap
        if eng == "act":
            self.op("act", lambda e: e.copy(oa, ia), [in_], [out])
        else:
            self.op(eng, lambda e: e.tensor_copy(oa, ia), [in_], [out])

    def memset(self, eng, out, val):
        oa = out.ap
        self.op(eng, lambda e: e.memset(oa, val), [], [out])

    def red(self, eng, out, in_, op=ALU.add):
        oa, ia = out.ap, in_.ap
        self.op(eng, lambda e: e.tensor_reduce(oa, ia, AX.X, op), [in_], [out])

    def emit(self):
        nc = self.nc
        sems = {}
        for e in ENGS:
            if e != "sp":
                sems[e] = self.es.enter_context(nc.semaphore(f"s_{e}"))
        for slot in self.slot_val:
            sems[slot] = self.es.enter_context(nc.semaphore(f"d_{slot[0]}{slot[1]}"))
        final = [(sems[s], v) for s, v in self.slot_val.items()]
        lists = self.lists

        waited = {e: set() for e in ENGS}
        for e in ENGS:
            for waits, fn, kk, n in lists[e]:
                for key, val in waits:
                    if key in waited:
                        waited[key].add(val)
        rank = {e: {n: i + 1 for i, n in enumerate(sorted(waited[e]))} for e in ENGS}

        ctr = [0]

        def replay(name, eng, tail=False):
            for waits, fn, kk, n in lists[name]:
                for key, val in waits:
                    if key in rank:
                        val = rank[key][val]
                    eng.wait_ge(sems[key], val)
                    ctr[0] += 1
                if fn is None:
                    continue
                k, inc = kk
                ins = fn(eng)
                if DEBUG_LABELS:
                    LABELS[ctr[0]] = (name, getattr(fn, "_lbl", "?"), str(ins)[:300])
                ctr[0] += 1
                if n is None:
                    ins.then_inc(sems[k], inc)
                elif n in rank[k]:
                    ins.then_inc(sems[k], 1)
            if tail:
                for s_, v in final:
                    eng.wait_ge(s_, v)

        with nc.Block() as block:
            @block.sync
            def _(e):
                replay("sp", e, tail=True)

            @block.tensor
            def _(e):
                replay("pe", e)

            @block.scalar
            def _(e):
                replay("act", e)

            @block.vector
            def _(e):
                replay("dve", e)

            @block.gpsimd
            def _(e):
                replay("pool", e)
        self.es.close()


def build(TH, dbg=False):
    nc = bass.Bass("TRN2", target_bir_lowering=False)
    P = Prog(nc)
    NB = TH // TB
    TT = 2 * TH

    def ext(name, shape, dt=F32, kind="ExternalInput"):
        return P.dram(name, shape, dt, kind=kind)

    xp = ext("xp", [TT + 4, D])
    w_in = ext("w_in", [D, NCOL])
    lbl = ext("lbl", [4, 1024])
    conv5 = ext("conv5", [3072, 5])
    alog = ext("alog", [16])
    dtb = ext("dtb", [16])
    hgnw = ext("hgnw", [128])
    gdnw = ext("gdnw", [128])
    n1w = ext("n1w", [D])
    n2w = ext("n2w", [D])
    fnw = ext("fnw", [D])
    w_out = ext("w_out", [D, D])
    w1 = ext("w1", [D, DFF])
    w2 = ext("w2", [DFF, D])
    consts = ext("consts", [128, NCONST])
    out = ext("out", [TH, D], kind="ExternalOutput")

    hgw = {d: [P.dram(f"hgw{d}{h}", [128, 16, 512], BF16) for h in range(8)] for d in "AB"}
    gdw = [P.dram(f"gdw{h}", [128, 16, 512], BF16) for h in range(8)]
    bgw = {d: P.dram(f"bgw{d}", [128, 16, 16], BF16) for d in "AB"}
    wog = [P.dram(f"wog{i}", [128, 16, 256], BF16) for i in range(8)]
    w1g = [P.dram(f"w1g{i}", [128, 16, 256], BF16) for i in range(32)]
    w2g = [[P.dram(f"w2g{cb}_{fg}", [128, 8, 512], BF16) for fg in range(8)] for cb in range(4)]
    ost = P.dram("ost", [TH, D], F32)
    mixs = P.dram("mixs", [TH, D], BF16)

    win_v = w_in.re("(kt p) c -> p kt c", p=128)

    def conv_cols(dst, j, c0, n=128):
        P.dma("pool", dst[:, :, j * 128:j * 128 + n], win_v[:, :, c0:c0 + n])

    for h in range(8):
        for d, fo in (("A", 1024), ("B", 2048)):
            conv_cols(hgw[d][h], 0, 0 + h * 128)
            conv_cols(hgw[d][h], 1, fo + h * 128)
            conv_cols(hgw[d][h], 2, 3072 + h * 128)
            conv_cols(hgw[d][h], 3, 4096 + h * 128)
    for h in range(8):
        for j, base in enumerate((5120, 6144, 7168, 8192)):
            conv_cols(gdw[h], j, base + h * 128)
    for d, o in (("A", 0), ("B", 8)):
        P.dma("pool", bgw[d][:, :, 0:8], win_v[:, :, 9216 + o:9216 + o + 8])
        P.dma("pool", bgw[d][:, :, 8:16], win_v[:, :, 9232 + o:9232 + o + 8])
    late_conv = []
    wo_v = w_out.re("(kt p) c -> p kt c", p=128)
    for cb in range(8):
        late_conv.append((wog[cb], wo_v[:, :, cb * 256:(cb + 1) * 256]))
    w1_v = w1.re("(kt p) f -> p kt f", p=128)
    for fg in range(32):
        late_conv.append((w1g[fg], w1_v[:, :, fg * 256:(fg + 1) * 256]))
    w2_v = w2.re("(ft p) c -> p ft c", p=128)
    for cb in range(4):
        for fg in range(8):
            late_conv.append((w2g[cb][fg], w2_v[:, fg * 8:(fg + 1) * 8, cb * 512:(cb + 1) * 512]))

    psb = []
    for i in range(8):
        t = P.es.enter_context(nc.psum_tensor(f"ps{i}", [128, 512], F32))
        psb.append(t)
    bigs = Rot([V(psb[i][:], [Buf(True)]) for i in range(2)])
    qbufs = [Buf(True) for _ in range(5)]
    quarters = Rot([V(psb[2 + i % 5][:, (i // 5) * 128:(i // 5 + 1) * 128], [qbufs[i % 5]]) for i in range(20)])
    ops_slot = V(psb[7][:, 0:128], [Buf(True)])

    C = P.sb("C", [128, NCONST], F32)
    P.dma("sp", C, consts)
    ident = C[:, C_ID:C_ID + 128]
    ones = C[:, C_ONES:C_ONES + 128]
    identb = P.sb("identb", [128, 128], BF16)
    P.cp("dve", identb, ident)
    c128 = P.sb("c128", [128, 128], F32)
    P.ts("dve", c128, ones, 128.0, None, ALU.mult)

    def CD(d, off):
        b = C_DIR[d] + off
        return C[:, b:b + 128]

    P.arena_init(43600)
    n1w_b = P.al([128, D], F32)
    P.dma("sp", n1w_b, V(n1w.ap.partition_broadcast(128), n1w.bufs))
    oml = P.al([128, 2, 1024], F32)
    xts_l = P.sbn("xt", [128, D], F32, 2)
    for di in range(2):
        lt = xts_l[di]
        for r in range(2):
            P.dma("sp", lt[:, r * 1024:(r + 1) * 1024],
                  V(lbl.ap[2 * di + r:2 * di + r + 1, :].to_broadcast([128, 1024]), lbl.bufs))
        P.tt("dve", lt[:, 0:1024], lt[:, 0:1024], lt[:, 1024:2048], ALU.subtract)
        P.act(lt[:, 1024:2048], lt[:, 0:1024], AF.Exp)
        P.ts("dve", oml[:, di, :], lt[:, 1024:2048], 1.0, -1.0, ALU.add, ALU.pow)
    cw = P.al([128, 24, 5], F32)
    P.dma("sp", cw, conv5.re("(g p) j -> p g j", p=128))
    dtb_b = P.al([128, 16], F32)
    P.dma("sp", dtb_b, V(dtb.ap.partition_broadcast(128), dtb.bufs))
    nA_b = P.al([128, 16], F32)
    P.dma("sp", nA_b, V(alog.ap.partition_broadcast(128), alog.bufs))
    P.act(nA_b, nA_b, AF.Exp)
    P.ts("dve", nA_b, nA_b, -1.0, None, ALU.mult)
    hgnw_b = P.al([128, 128], F32)
    P.dma("sp", hgnw_b, V(hgnw.ap.partition_broadcast(128), hgnw.bufs))
    gdnw_b = P.al([128, 128], F32)
    P.dma("sp", gdnw_b, V(gdnw.ap.partition_broadcast(128), gdnw.bufs))

    dbg_out = {}

    def dbg_dump(name, v, shape, dt=F32):
        if not dbg:
            return
        if name in dbg_out:
            return
        t = P.dram("dbg_" + name, shape, dt, kind="ExternalOutput")
        dbg_out[name] = t
        P.dma("sp", t, v)

    xts = Rot(xts_l)
    junk = P.sb("junk", [128, D], BF16)
    hbs = Rot(P.sbn("hb", [128, D], BF16, 2))
    stat = Rot(P.sbn("stat", [128, 4], F32, 4))

    def norm_rows(xt, np_, wb):
        st = stat.next()
        hb = hbs.next()
        P.act(junk[0:np_, :], xt[0:np_, :], AF.Square, accum=st[0:np_, 0:1])
        P.ts("dve", st[0:np_, 1:2], st[0:np_, 0:1], 1.0 / D, EPS, ALU.mult, ALU.add)
        P.ts("dve", st[0:np_, 2:3], st[0:np_, 1:2], -0.5, None, ALU.pow)
        P.stt("dve", hb[0:np_, :], xt[0:np_, :], st[0:np_, 2:3], wb[0:np_, :], ALU.mult, ALU.mult)
        return hb, st

    def transpose_rows(hb, np_, dstT, c0):
        for g in range(4):
            ps = quarters.next()
            psv = ps.bitcast(BF16)
            if np_ > 64:
                ps = bigs.next()
                psv = ps.bitcast(BF16)[:, 0:512]
            pv = psv[:, 0:4 * np_].re("p (a b) -> p a b", a=4)
            for j in range(4):
                kt = g * 4 + j
                P.tr(pv[:, j, :], hb[0:np_, kt * 128:(kt + 1) * 128], identb[0:np_, 0:np_])
            P.cp("act" if g % 2 == 0 else "dve", dstT[:, g * 4:(g + 1) * 4, c0:c0 + np_], pv)

    hTs = Rot(P.aln([128, 16, TB + 4], BF16, 1))

    def make_hT(s0):
        hT = hTs.next()
        for i in range(5):
            np_ = 128 if i < 4 else 4
            xt = xts.next()
            P.dma("sp", xt[0:np_, :], xp[s0 + 128 * i:s0 + 128 * i + np_, :])
            hb, _ = norm_rows(xt, np_, n1w_b)
            transpose_rows(hb, np_, hT, 128 * i)
        return hT

    Sf = P.aln([128, 128], F32, 16)
    Sb = P.aln([128, 128], BF16, 16)
    wbufs = Rot(P.aln([128, 16, 512], BF16, 2))
    wbg_sb = {d: P.al([128, 16, 16], BF16) for d in "AB"}
    for d in "AB":
        P.dma("sp", wbg_sb[d], bgw[d])

    plan = [("A", b * TB) for b in range(NB)] + [("F", b * TB) for b in range(2 * NB - 1, NB - 1, -1)] + \
           [("B", b * TB) for b in range(NB - 1, -1, -1)]
    wseq = []
    for mode_, _s in plan:
        d_ = "A" if mode_ == "A" else "B"
        wseq += [hgw[d_][h] for h in range(8)] + [gdw[h] for h in range(8)]
    wstate = {"k": 0, "bufs": [wbufs.next(), wbufs.next()]}

    def wnext():
        k = wstate["k"]
        if late_conv and k >= 4:
            dst_, src_ = late_conv.pop(0)
            P.dma("pool", dst_, src_)
        if k == 0:
            P.dma("sp", wstate["bufs"][0], wseq[0])
        if k + 1 < len(wseq):
            P.dma("sp", wstate["bufs"][(k + 1) % 2], wseq[k + 1])
        wstate["k"] = k + 1
        return wstate["bufs"][k % 2]

    def zero_states():
        for h in range(16):
            P.memset("pool", Sf[h], 0.0)
            P.memset("pool", Sb[h], 0.0)

    w512 = Rot(P.aln([128, 512], F32, 10))
    w516 = Rot(P.aln([128, 516], F32, 3))
    keep512 = Rot(P.aln([128, 512], F32, 6))
    w128 = Rot(P.aln([128, 128], F32, 28))
    tset = [{"f": P.aln([128, 128], F32, 13), "b": P.aln([128, 128], BF16, 6)} for _ in range(4)]
    b128 = Rot(P.aln([128, 128], BF16, 8))
    sm = Rot(P.aln([128, 16], F32, 24))
    bsm = Rot(P.aln([128, 16], F32, 20))

    def silu_from(dst, src_v, scale=1.0):
        e = w512.next()[:, 0:src_v.ap.shape[1]]
        P.act(e, src_v, AF.Exp, scale=-1.0)
        P.ts("dve", e, e, 1.0, -1.0, ALU.add, ALU.pow)
        P.stt("dve", dst, src_v, scale, e, ALU.mult, ALU.mult)

    def finalize_head(h, r0, o_ps_or_sb, gate_logits, nw_b):
        oa = w128.next()
        P.dma("sp", oa, ost[r0:r0 + 128, h * 128:(h + 1) * 128])
        ot = w128.next()
        P.tt("dve", ot, oa, o_ps_or_sb, ALU.add)
        st = sm.next()
        jk = w128.next()
        P.act(jk, ot, AF.Square, accum=st[:, 0:1])
        P.ts("dve", st[:, 1:2], st[:, 0:1], 1.0 / 128, EPS, ALU.mult, ALU.add)
        P.ts("dve", st[:, 2:3], st[:, 1:2], -0.5, None, ALU.pow)
        on = w128.next()
        P.stt("dve", on, ot, st[:, 2:3], nw_b, ALU.mult, ALU.mult)
        gs = w128.next()
        silu_from(gs, gate_logits)
        mx = b128.next()
        P.tt("pool", mx, on, gs, ALU.mult)
        P.dma("pool", mixs[r0:r0 + 128, h * 128:(h + 1) * 128], mx)

    def mixer_block(mode, s0):
        d = "A" if mode == "A" else "B"
        di = 0 if d == "A" else 1
        full = mode != "F"
        asc = d == "A"
        hT = make_hT(s0)
        tiles = list(range(4)) if asc else list(range(3, -1, -1))
        U2, M1, M2, UD, VD = (CD(d, o) for o in (C_U2, C_M1, C_M2, C_UD, C_VD))
        md16 = C[:, C_MD16:C_MD16 + 128]

        for h in range(8):
            wb = wnext()
            S_f, S_b = Sf[h], Sb[h]
            if full:
                qps = bigs.next()
                for kt in range(16):
                    P.mm(qps, wb[:, kt, 0:128], hT[:, kt, 2:2 + TB], start=(kt == 0), stop=(kt == 15))
                qs = keep512.next()
                silu_from(qs, qps, scale=128.0 ** -0.5)
            ncol = 384 if mode == "B" else 256
            for tiles_g in [tiles[k:k + GRP_HG] for k in range(0, 4, GRP_HG)]:
                R = {}
                for i in tiles_g:
                    f, bb = tset[i]["f"], tset[i]["b"]
                    tps = bigs.next()
                    for kt in range(16):
                        P.mm(tps[:, 0:ncol], hT[:, kt, 2 + 128 * i:2 + 128 * i + 128], wb[:, kt, 128:128 + ncol],
                             start=(kt == 0), stop=(kt == 15))
                    P.act(f[0], tps[:, 0:128], AF.Exp, scale=-1.0)
                    P.cp("act", bb[0], tps[:, 128:256])
                    if mode == "B":
                        P.cp("act", f[8], tps[:, 256:384])
                for i in tiles_g:
                    f = tset[i]["f"]
                    P.ts("dve", f[1], f[0], 1.0, -1.0, ALU.add, ALU.pow)
                for i in tiles_g:
                    f = tset[i]["f"]
                    P.tt("dve", f[2], f[0], f[1], ALU.mult)
                    P.tt("pool", f[2], f[2], oml[:, di, h * 128:(h + 1) * 128], ALU.mult)
                for i in tiles_g:
                    f = tset[i]["f"]
                    P.act(f[3], f[2], AF.Ln, bias=1.0, scale=-1.0)
                for i in tiles_g:
                    f = tset[i]["f"]
                    r_ = {}
                    r_["b"] = quarters.next()
                    P.mm(r_["b"], f[3], U2)
                    r_["rem"] = quarters.next()
                    P.mm(r_["rem"], M2, f[3])
                    if full:
                        r_["bmr"] = quarters.next()
                        P.mm(r_["bmr"], f[3], M1)
                        r_["kT"] = quarters.next()
                        P.tr(r_["kT"], f[2], ident)
                    R[i] = r_
                for i in tiles_g:
                    f, bb = tset[i]["f"], tset[i]["b"]
                    P.act(f[4], R[i]["b"], AF.Exp)
                    P.act(f[5], R[i]["rem"], AF.Exp)
                    if full:
                        P.act(f[6], R[i]["bmr"], AF.Exp)
                        P.act(f[7], R[i]["bmr"], AF.Exp, scale=-1.0)
                for i in tiles_g:
                    f, bb = tset[i]["f"], tset[i]["b"]
                    P.tt("dve", bb[1], f[2], f[5], ALU.mult)
                    if full:
                        qsl = qs[:, 128 * i:128 * i + 128]
                        P.tt("pool", bb[2], qsl, f[6], ALU.mult)
                        P.tt("dve", bb[3], R[i]["kT"], f[7], ALU.mult)
                        P.tt("pool", bb[4], qsl, f[4], ALU.mult)
                if full:
                    for i in tiles_g:
                        bb = tset[i]["b"]
                        R[i]["AT"] = quarters.next()
                        P.mm(R[i]["AT"], bb[3], bb[2])
                    for i in tiles_g:
                        bb = tset[i]["b"]
                        P.tt("dve", bb[5], R[i]["AT"], U2, ALU.mult)
                for i in tiles_g:
                    f, bb = tset[i]["f"], tset[i]["b"]
                    v_bf, kst, qin, ATm, Ec = bb[0], bb[1], bb[4], bb[5], f[4]
                    if full:
                        o_ps = ops_slot
                        P.mm(o_ps, ATm, v_bf, start=True, stop=False)
                    chunks = (0, 1) if asc else (1, 0)
                    for ci, c in enumerate(chunks):
                        rs = slice(64 * c, 64 * c + 64)
                        if full:
                            P.mm(o_ps[rs, :], qin[:, rs], S_b, start=False, stop=True)
                        kv_ps = quarters.next()
                        P.mm(kv_ps, kst[rs, :], v_bf[rs, :])
                        dcol = 64 * c + 63 if asc else 64 * c
                        P.stt("dve", S_f, S_f, Ec[:, dcol:dcol + 1], kv_ps, ALU.mult, ALU.add)
                        P.cp("act", S_b, S_f)
                    if full:
                        r0 = s0 + 128 * i
                        if mode == "A":
                            osb = w128.next()
                            P.cp("act", osb, o_ps)
                            P.dma("pool", ost[r0:r0 + 128, h * 128:(h + 1) * 128], osb)
                        else:
                            finalize_head(h, r0, o_ps, f[8], hgnw_b)

        bet, gg, egc, dch, bega = {}, {}, {}, {}, {}
        for i in range(4):
            bps = quarters.next()
            for kt in range(16):
                P.mm(bps[:, 0:16], hT[:, kt, 2 + 128 * i:2 + 128 * i + 128], wbg_sb[d][:, kt, :],
                     start=(kt == 0), stop=(kt == 15))
            t1 = sm.next()
            P.act(t1[:, 0:8], bps[:, 0:8], AF.Exp, scale=-1.0)
            be = bsm.next()
            P.ts("dve", be[:, 0:8], t1[:, 0:8], 1.0, -1.0, ALU.add, ALU.pow)
            z = sm.next()
            P.tt("dve", z[:, 0:8], bps[:, 8:16], dtb_b[:, 8 * di:8 * di + 8], ALU.add)
            P.act(z[:, 0:8], z[:, 0:8], AF.Exp)
            P.act(z[:, 8:16], z[:, 0:8], AF.Ln, bias=1.0)
            g_ = bsm.next()
            P.tt("dve", g_[:, 0:8], z[:, 8:16], nA_b[:, 8 * di:8 * di + 8], ALU.mult)
            cps = quarters.next()
            P.mm(cps[:, 0:8], UD, g_[:, 0:8])
            P.mm(cps[:, 8:16], VD, g_[:, 0:8])
            P.mm(cps[:, 16:24], ones, g_[:, 0:8])
            ex = bsm.next()
            P.act(ex[:, 0:8], cps[:, 0:8], AF.Exp)
            P.act(ex[:, 8:16], cps[:, 8:16], AF.Exp)
            dc = bsm.next()
            P.act(dc[:, 0:8], cps[:, 16:24], AF.Exp)
            bg_ = bsm.next()
            P.tt("dve", bg_[:, 0:8], be[:, 0:8], ex[:, 0:8], ALU.mult)
            P.ts("dve", bg_[:, 8:16], be[:, 0:8], -1.0, None, ALU.mult)
            bet[i], gg[i], egc[i], dch[i], bega[i] = be, g_, ex, dc, bg_

        for h in range(8):
            wb = wnext()
            S_f, S_b = Sf[8 + h], Sb[8 + h]
            outs = {}
            names = [nm for nm in "qkv" if not (nm == "q" and not full)]
            jj = {"q": 0, "k": 1, "v": 2}
            us, ysd, yd = {}, {}, {}
            for nm in names:
                j = jj[nm]
                pa = bigs.next()
                pb = quarters.next()
                for kt in range(16):
                    P.mm(pa, wb[:, kt, j * 128:(j + 1) * 128], hT[:, kt, 0:TB], start=(kt == 0), stop=(kt == 15))
                for kt in range(16):
                    P.mm(pb[:, 0:4], wb[:, kt, j * 128:(j + 1) * 128], hT[:, kt, TB:TB + 4],
                         start=(kt == 0), stop=(kt == 15))
                u = w516.next()
                P.cp("act", u[:, 0:TB], pa)
                P.cp("act", u[:, TB:TB + 4], pb[:, 0:4])
                us[nm] = u
            for nm in names:
                y = w512.next()
                g24 = jj[nm] * 8 + h
                u = us[nm]
                P.ts("dve", y, u[:, 0:TB], cw[:, g24, 0:1], None, ALU.mult)
                for tp in range(1, 5):
                    P.stt("dve", y, u[:, tp:tp + TB], cw[:, g24, tp:tp + 1], y, ALU.mult, ALU.add)
                yd[nm] = y
            for nm in names:
                ys = keep512.next() if nm == "v" else w512.next()
                silu_from(ys, yd[nm])
                ysd[nm] = ys
            outs["v"] = ysd["v"]
            sqs, spss = {}, {}
            for nm in names:
                if nm == "v":
                    continue
                sqs[nm] = w512.next()
                P.act(sqs[nm], ysd[nm], AF.Square)
            for nm in names:
                if nm == "v":
                    continue
                spss[nm] = bigs.next()
                P.mm(spss[nm], c128 if nm == "q" else ones, sqs[nm])
            for nm in names:
                if nm == "v":
                    continue
                r = w512.next()
                P.ts("dve", r, spss[nm], (128.0 if nm == "q" else 1.0) * EPS, -0.5, ALU.add, ALU.pow)
                nrm = keep512.next()
                P.tt("pool", nrm, ysd[nm], r, ALU.mult)
                outs[nm] = nrm
            knT, vT = outs["k"], outs["v"]
            qnT = outs.get("q")
            for tiles_g in [tiles[k:k + GRP_GDN] for k in range(0, 4, GRP_GDN)]:
                R = {i: {} for i in tiles_g}
                for i in tiles_g:
                    cs = slice(128 * i, 128 * i + 128)
                    R[i]["ktm"] = quarters.next()
                    P.tr(R[i]["ktm"], knT[:, cs], ident)
                    R[i]["vtm"] = quarters.next()
                    P.tr(R[i]["vtm"], vT[:, cs], ident)
                for i in tiles_g:
                    f, bb = tset[i]["f"], tset[i]["b"]
                    P.amul(f[0], R[i]["ktm"], bega[i][:, h:h + 1])
                    P.amul(f[1], R[i]["vtm"], bet[i][:, h:h + 1])
                    P.amul(bb[0], R[i]["ktm"], egc[i][:, 8 + h:9 + h])
                    R[i]["GU"] = w128.next()
                    P.ts("dve", R[i]["GU"], UD, gg[i][:, h:h + 1], None, ALU.mult)
                    if full:
                        R[i]["GV"] = w128.next()
                        P.ts("dve", R[i]["GV"], VD, gg[i][:, h:h + 1], None, ALU.mult)
                for i in tiles_g:
                    cs = slice(128 * i, 128 * i + 128)
                    R[i]["D1"] = quarters.next()
                    P.mm(R[i]["D1"], R[i]["GU"], VD)
                    R[i]["G"] = quarters.next()
                    P.mm(R[i]["G"], knT[:, cs], knT[:, cs])
                for i in tiles_g:
                    R[i]["E1"] = w128.next()
                    P.act(R[i]["E1"], R[i]["D1"], AF.Exp)
                for i in tiles_g:
                    f = tset[i]["f"]
                    P.stt("dve", f[2], R[i]["G"], bega[i][:, 8 + h:9 + h], R[i]["E1"], ALU.mult, ALU.mult)
                    P.tt("pool", f[2], f[2], VD, ALU.mult)
                if full:
                    for i in tiles_g:
                        cs = slice(128 * i, 128 * i + 128)
                        R[i]["D2"] = quarters.next()
                        P.mm(R[i]["D2"], R[i]["GV"], UD)
                        R[i]["KQ"] = quarters.next()
                        P.mm(R[i]["KQ"], knT[:, cs], qnT[:, cs])
                    for i in tiles_g:
                        R[i]["E2"] = w128.next()
                        P.act(R[i]["E2"], R[i]["D2"], AF.Exp)
                    for i in tiles_g:
                        cs = slice(128 * i, 128 * i + 128)
                        bb = tset[i]["b"]
                        qk0 = w128.next()
                        P.tt("dve", qk0, R[i]["KQ"], R[i]["E2"], ALU.mult)
                        P.tt("pool", bb[1], qk0, UD, ALU.mult)
                        P.cp("pool", bb[2], qnT[:, cs])
                for i in tiles_g:
                    R[i]["XTp"] = quarters.next()
                    P.tr(R[i]["XTp"], tset[i]["f"][2], ident)
                for i in tiles_g:
                    f = tset[i]["f"]
                    P.cp("act", f[3], R[i]["XTp"])
                for i in tiles_g:
                    f = tset[i]["f"]
                    P.tt("pool", f[4], f[2], md16, ALU.mult)
                    P.tt("pool", f[6], f[3], md16, ALU.mult)
                    P.tt("dve", f[8], f[4], ident, ALU.add)
                    P.tt("dve", f[10], f[6], ident, ALU.add)
                cur = 0
                for lev in range(3):
                    nx = 1 - cur
                    for i in tiles_g:
                        f = tset[i]["f"]
                        R[i]["y2p"] = quarters.next()
                        P.mm(R[i]["y2p"], f[6 + cur], f[4 + cur])
                        R[i]["yt2p"] = quarters.next()
                        P.mm(R[i]["yt2p"], f[4 + cur], f[6 + cur])
                    for i in tiles_g:
                        f = tset[i]["f"]
                        P.cp("act", f[4 + nx], R[i]["y2p"])
                        P.cp("act", f[6 + nx], R[i]["yt2p"])
                    for i in tiles_g:
                        f = tset[i]["f"]
                        R[i]["tp"] = quarters.next()
                        P.mm(R[i]["tp"], f[6 + nx], f[8 + cur])
                        R[i]["ttp"] = quarters.next()
                        P.mm(R[i]["ttp"], f[4 + nx], f[10 + cur])
                    for i in tiles_g:
                        f = tset[i]["f"]
                        P.tt("dve", f[8 + nx], R[i]["tp"], f[8 + cur], ALU.add)
                        P.tt("dve", f[10 + nx], R[i]["ttp"], f[10 + cur], ALU.add)
                    cur = nx
                for B_ in (32, 64, 128):
                    nx = 1 - cur
                    mo = C[:, C_MOFF[B_]:C_MOFF[B_] + 128]
                    for i in tiles_g:
                        f = tset[i]["f"]
                        R[i]["N"] = w128.next()
                        P.tt("pool", R[i]["N"], f[2], mo, ALU.mult)
                        R[i]["NT"] = w128.next()
                        P.tt("pool", R[i]["NT"], f[3], mo, ALU.mult)
                    for i in tiles_g:
                        f = tset[i]["f"]
                        R[i]["ap"] = quarters.next()
                        P.mm(R[i]["ap"], R[i]["N"], f[10 + cur])
                        if B_ < 128:
                            R[i]["bp"] = quarters.next()
                            P.mm(R[i]["bp"], R[i]["NT"], f[8 + cur])
                    for i in tiles_g:
                        R[i]["A"] = w128.next()
                        P.cp("act", R[i]["A"], R[i]["ap"])
                        if B_ < 128:
                            R[i]["B"] = w128.next()
                            P.cp("act", R[i]["B"], R[i]["bp"])
                    for i in tiles_g:
                        f = tset[i]["f"]
                        R[i]["ttp"] = quarters.next()
                        P.mm(R[i]["ttp"], f[8 + cur], R[i]["A"])
                        if B_ < 128:
                            R[i]["tp"] = quarters.next()
                            P.mm(R[i]["tp"], f[10 + cur], R[i]["B"])
                    for i in tiles_g:
                        f = tset[i]["f"]
                        P.tt("dve", f[10 + nx], R[i]["ttp"], f[10 + cur], ALU.add)
                        if B_ < 128:
                            P.tt("dve", f[8 + nx], R[i]["tp"], f[8 + cur], ALU.add)
                    cur = nx
                for i in tiles_g:
                    f = tset[i]["f"]
                    R[i]["wT"] = quarters.next()
                    P.mm(R[i]["wT"], f[0], f[10 + cur])
                    R[i]["u"] = quarters.next()
                    P.mm(R[i]["u"], f[10 + cur], f[1])
                for i in tiles_g:
                    f, bb = tset[i]["f"], tset[i]["b"]
                    P.cp("act", bb[3], R[i]["wT"])
                    P.cp("act", f[12], R[i]["u"])
                for i in tiles_g:
                    f, bb = tset[i]["f"], tset[i]["b"]
                    kst, qkT, qn_bf, wT_bf, vnew, u_sb = bb[0], bb[1], bb[2], bb[3], bb[4], f[12]
                    ws_ps = quarters.next()
                    P.mm(ws_ps, wT_bf, S_b)
                    P.tt("dve", vnew, u_sb, ws_ps, ALU.subtract)
                    if full:
                        o1 = quarters.next()
                        P.mm(o1, qn_bf, S_b)
                        o2 = quarters.next()
                        P.mm(o2, qkT, vnew)
                    kv_ps = quarters.next()
                    P.mm(kv_ps, kst, vnew)
                    P.stt("dve", S_f, S_f, dch[i][:, h:h + 1], kv_ps, ALU.mult, ALU.add)
                    P.cp("act", S_b, S_f)
                    if full:
                        o1s = w128.next()
                        P.amul(o1s, o1, egc[i][:, h:h + 1])
                        osb = w128.next()
                        P.tt("dve", osb, o1s, o2, ALU.add)
                        r0 = s0 + 128 * i
                        hh = 8 + h
                        if mode == "A":
                            P.dma("pool", ost[r0:r0 + 128, hh * 128:(hh + 1) * 128], osb)
                        else:
                            zps = bigs.next()
                            for kt in range(16):
                                P.mm(zps[:, 0:128], hT[:, kt, 2 + 128 * i:2 + 128 * i + 128], wb[:, kt, 384:512],
                                     start=(kt == 0), stop=(kt == 15))
                            finalize_head(hh, r0, osb, zps[:, 0:128], gdnw_b)

    for pi, (mode_, s0_) in enumerate(plan):
        if pi == 0 or pi == NB:
            zero_states()
        mixer_block(mode_, s0_)

    while late_conv:
        dst_, src_ = late_conv.pop(0)
        P.dma("pool", dst_, src_)
    P.barrier()
    P.aoff = 0
    bigs.items = [V(psb[i][:], [Buf(True)]) for i in range(8)]
    bigs.i = 0
    n2w_b = P.al([128, D], F32)
    P.dma("sp", n2w_b, V(n2w.ap.partition_broadcast(128), n2w.bufs))
    fnw_b = P.al([128, D], F32)
    P.dma("sp", fnw_b, V(fnw.ap.partition_broadcast(128), fnw.bufs))
    mixT = P.al([128, 16, TB], BF16)
    h2T = mixT
    hidT = P.al([128, 64, TB], BF16)
    x1 = P.aln([128, D], F32, 4)
    mts = hbs
    wo_sb = Rot(P.aln([128, 16, 256], BF16, 2))
    w2_sb = Rot(P.aln([128, 8, 512], BF16, 2))
    rl = Rot(P.aln([128, 512], F32, 3))

    for b in range(NB):
        s0 = b * TB
        for i in range(4):
            mt = mts.next()
            P.dma("sp", mt, mixs[s0 + 128 * i:s0 + 128 * i + 128, :])
            transpose_rows(mt, 128, mixT, 128 * i)
        for i in range(4):
            P.dma("sp", x1[i], xp[2 + s0 + 128 * i:2 + s0 + 128 * i + 128, :])
        for cb in range(8):
            wo = wo_sb.next()
            P.dma("sp", wo, wog[cb])
            for i in range(4):
                ps = bigs.next()
                for kt in range(16):
                    P.mm(ps[:, 0:256], mixT[:, kt, 128 * i:128 * i + 128], wo[:, kt, :], start=(kt == 0), stop=(kt == 15))
                P.tt("dve", x1[i][:, cb * 256:(cb + 1) * 256], x1[i][:, cb * 256:(cb + 1) * 256], ps[:, 0:256], ALU.add)
        for i in range(4):
            hb, _ = norm_rows(x1[i], 128, n2w_b)
            transpose_rows(hb, 128, h2T, 128 * i)
        for fg in range(32):
            w1s = wo_sb.next()
            P.dma("sp", w1s, w1g[fg])
            for f in range(2):
                ps = bigs.next()
                for kt in range(16):
                    P.mm(ps, w1s[:, kt, f * 128:(f + 1) * 128], h2T[:, kt, :], start=(kt == 0), stop=(kt == 15))
                r = rl.next()
                P.act(r, ps, AF.Relu)
                P.tt("dve" if f % 2 else "pool", hidT[:, fg * 2 + f, :], r, r, ALU.mult)
        for cb in range(4):
            accs = [bigs.next() for _ in range(4)]
            for fg in range(8):
                w2s = w2_sb.next()
                P.dma("sp", w2s, w2g[cb][fg])
                for i in range(4):
                    for f in range(8):
                        ft = fg * 8 + f
                        P.mm(accs[i], hidT[:, ft, 128 * i:128 * i + 128], w2s[:, f, :], start=(ft == 0), stop=(ft == 63))
            for i in range(4):
                P.tt("dve", x1[i][:, cb * 512:(cb + 1) * 512], x1[i][:, cb * 512:(cb + 1) * 512], accs[i], ALU.add)
        for i in range(4):
            st = stat.next()
            xo = xts.next()
            P.act(junk, x1[i], AF.Square, accum=st[:, 0:1])
            P.ts("dve", st[:, 1:2], st[:, 0:1], 1.0 / D, EPS, ALU.mult, ALU.add)
            P.ts("dve", st[:, 2:3], st[:, 1:2], -0.5, None, ALU.pow)
            P.stt("dve", xo, x1[i], st[:, 2:3], fnw_b, ALU.mult, ALU.mult)
            P.dma("pool", out[s0 + 128 * i:s0 + 128 * i + 128, :], xo)

    P.emit()
    return nc, list(dbg_out.keys())


_NC_CACHE = {}


def _prep_core_inputs(inp, core, TH):
    b, half = core // 2, core % 2
    T = 2 * TH
    x = np.asarray(inp["x"], np.float32)[b, :T]
    w_in = np.asarray(inp["w_in"], np.float32)[0]
    lbf = np.asarray(inp["hg_lb_logits_fwd"], np.float32)
    lbb = np.asarray(inp["hg_lb_logits_bwd"], np.float32)
    cw = np.asarray(inp["gdn_conv_w"], np.float32)[0]
    alog = np.asarray(inp["gdn_a_log"], np.float32)[0]
    dtb = np.asarray(inp["gdn_dt_bias"], np.float32)[0]
    z1 = np.zeros((1, 3072), np.float32)
    if half == 0:
        xs = x
        wi = w_in
        lbl = np.concatenate([lbf, lbb], 0)
        c5 = np.concatenate([cw, z1], 0)
        al, dt = alog, dtb
    else:
        xs = x[::-1]
        wi = w_in.copy()
        wi[:, 1024:2048] = w_in[:, 2048:3072]
        wi[:, 2048:3072] = w_in[:, 1024:2048]
        wi[:, 9216:9224] = w_in[:, 9224:9232]
        wi[:, 9224:9232] = w_in[:, 9216:9224]
        wi[:, 9232:9240] = w_in[:, 9240:9248]
        wi[:, 9240:9248] = w_in[:, 9232:9240]
        lbl = np.concatenate([lbb, lbf], 0)
        c5 = np.concatenate([z1, cw[::-1]], 0)
        al, dt = alog[::-1], dtb[::-1]
    xpad = np.zeros((T + 4, D), np.float32)
    xpad[2:T + 2] = xs
    return {
        "xp": xpad,
        "w_in": np.ascontiguousarray(wi),
        "lbl": np.ascontiguousarray(lbl),
        "conv5": np.ascontiguousarray(c5.T),
        "alog": np.ascontiguousarray(al).reshape(16),
        "dtb": np.ascontiguousarray(dt).reshape(16),
        "hgnw": np.asarray(inp["hg_norm_w"], np.float32)[0],
        "gdnw": np.asarray(inp["gdn_norm_w"], np.float32)[0],
        "n1w": np.asarray(inp["norm1_w"], np.float32)[0],
        "n2w": np.asarray(inp["norm2_w"], np.float32)[0],
        "fnw": np.asarray(inp["final_norm_w"], np.float32),
        "w_out": np.asarray(inp["w_out"], np.float32)[0],
        "w1": np.asarray(inp["w_mlp_in"], np.float32)[0],
        "w2": np.asarray(inp["w_mlp_out"], np.float32)[0],
        "consts": make_consts(),
    }


def run(inputs, TH, dbg=False):
    key = (TH, dbg)
    if key not in _NC_CACHE:
        _NC_CACHE[key] = build(TH, dbg)
    nc, dbg_names = _NC_CACHE[key]
    in_maps = [_prep_core_inputs(inputs, c, TH) for c in range(8)]
    res = run_bass_kernel_spmd(nc, in_maps, core_ids=list(range(8)))
    B = 4
    outp = np.zeros((B, 2 * TH, D), np.float32)
    for c in range(8):
        b, half = c // 2, c % 2
        o = np.asarray(res.results[c]["out"], np.float32)
        if half == 0:
            outp[b, :TH] = o
        else:
            outp[b, TH:] = o[::-1]
    return outp, res


def kernel(**inputs):
    x = np.asarray(inputs["x"])
    TH = x.shape[1] // 2
    outp, _ = run(inputs, TH)
    return outp
```
